# Optimizing a Trainium2 kernel written in Bass

```python
import jax, jax.numpy as jnp
from jax import lax
import numpy as np

D_MODEL = 1024
BATCH = 16
SEQ = 256
DEPTH = 1
DEC_BATCH = 8
DEC_SEQ = 2048
PAST_LEN = 512

GRID_W = 64
D_RNN = D_MODEL
LRU_BLOCKS = 16
LRU_BW = D_RNN // LRU_BLOCKS
CONV_W = 4
LRU_C = 8.0
GLA_HEADS = 4
GLA_DK = D_MODEL // 2 // GLA_HEADS
GLA_DV = D_MODEL // GLA_HEADS
GLA_DK_TOT = GLA_HEADS * GLA_DK
GLA_DV_TOT = GLA_HEADS * GLA_DV
GLA_RANK = 16
GLA_TAU = 16.0
GLA_CHUNK = 64
D_FF = 4 * D_MODEL
N_MOD = 6
EPS = 1e-6
IN_SPLITS = (D_RNN, D_RNN, GLA_DK_TOT, GLA_DK_TOT, GLA_DV_TOT, GLA_DV_TOT, 2 * GLA_RANK, D_MODEL, D_MODEL)
IN_TOTAL = D_RNN * 2 + GLA_DK_TOT * 2 + GLA_DV_TOT * 2 + 2 * GLA_RANK + 2 * D_MODEL

kernel_name = "hybrid_lru_gla_diffusion_step"


def rmsnorm(x, g):
    xf = x.astype(jnp.float32)
    y = xf * lax.rsqrt(jnp.mean(xf * xf, axis=-1, keepdims=True) + EPS)
    return (y * g.astype(jnp.float32)).astype(x.dtype)


def conv_centred(x, w, b):
    T = x.shape[-2]
    pad = [(0, 0)] * (x.ndim - 2) + [((CONV_W - 1) // 2, CONV_W // 2), (0, 0)]
    xp = jnp.pad(x, pad)
    return b + sum(xp[..., k:k + T, :] * w[k] for k in range(CONV_W))


def blockdiag(x, w):
    xb = x.reshape(*x.shape[:-1], LRU_BLOCKS, LRU_BW)
    return jnp.einsum('...nc,ncd->...nd', xb, w).reshape(x.shape)


def rg_lru_scan(xc, wa, ba, wx, bx, L, h0, reverse):
    if reverse:
        xc = jnp.flip(xc, axis=1)
    r = jax.nn.sigmoid(blockdiag(xc, wa) + ba)
    i = jax.nn.sigmoid(blockdiag(xc, wx) + bx)
    log_a = (-LRU_C * r * jax.nn.softplus(-L)).astype(jnp.float32)
    a = jnp.exp(log_a)
    u = jnp.sqrt(-jnp.expm1(2.0 * log_a)) * (i * xc).astype(jnp.float32)
    u = u.at[:, 0].add(a[:, 0] * h0.astype(jnp.float32))

    def comb(lft, rgt):
        al, bl = lft
        ar, br = rgt
        return al * ar, ar * bl + br

    _, h = lax.associative_scan(comb, (a, u), axis=1)
    final = h[:, -1]
    if reverse:
        h = jnp.flip(h, axis=1)
    return h.astype(xc.dtype), final.astype(xc.dtype)


def gla_chunked(q, k, v, log_a, s0):
    B, T, H, DK = q.shape
    DV = v.shape[-1]
    C = GLA_CHUNK
    N = T // C
    f32 = jnp.float32
    qc = q.astype(f32).reshape(B, N, C, H, DK) * (DK ** -0.5)
    kc = k.astype(f32).reshape(B, N, C, H, DK)
    vc = v.astype(f32).reshape(B, N, C, H, DV)
    bcum = jnp.cumsum(log_a.astype(f32).reshape(B, N, C, H, DK), axis=2)
    b_last = bcum[:, :, -1]
    q_e = qc * jnp.exp(bcum)
    k_e = kc * jnp.exp(-bcum)
    k_tail = kc * jnp.exp(b_last[:, :, None] - bcum)
    mask = jnp.tril(jnp.ones((C, C), dtype=bool))
    scores = jnp.where(mask, jnp.einsum('bnchk,bnshk->bnhcs', q_e, k_e), 0.0)
    o_intra = jnp.einsum('bnhcs,bnshv->bnchv', scores, vc)
    kv = jnp.einsum('bnshk,bnshv->bnhkv', k_tail, vc)
    decay = jnp.exp(b_last)

    def step(S, inp):
        d, upd = inp
        return d[..., None] * S + upd, S

    s_final, s_starts = lax.scan(step, s0.astype(f32), (jnp.moveaxis(decay, 1, 0), jnp.moveaxis(kv, 1, 0)))
    s_starts = jnp.moveaxis(s_starts, 0, 1)
    o_inter = jnp.einsum('bnchk,bnhkv->bnchv', q_e, s_starts)
    o = (o_intra + o_inter).reshape(B, T, H, DV)
    return o.astype(q.dtype), s_final.astype(q.dtype)


def to_col_major(t, rows):
    B, T = t.shape[:2]
    rest = t.shape[2:]
    return t.reshape(B, rows, GRID_W, *rest).swapaxes(1, 2).reshape(B, T, *rest)


def from_col_major(t, rows):
    B, T = t.shape[:2]
    rest = t.shape[2:]
    return t.reshape(B, GRID_W, rows, *rest).swapaxes(1, 2).reshape(B, T, *rest)


def token_mix(h, p, lru_h0, gla_s0, rows):
    B, T, _ = h.shape
    z = h @ p['w_in']
    splits = np.cumsum(IN_SPLITS)[:-1].tolist()
    zx, zg, q, k, v, g, lr, ga, gb = jnp.split(z, splits, axis=-1)

    if rows is None:
        xc = conv_centred(zx, p['conv_w'], p['conv_b'])
    else:
        xc = conv_centred(zx.reshape(B, rows, GRID_W, D_RNN), p['conv_w'], p['conv_b']).reshape(B, T, D_RNN)
    hf, sf = rg_lru_scan(xc, p['lru_wa'][0], p['lru_ba'][0], p['lru_wx'][0], p['lru_bx'][0], p['lru_L'][0], lru_h0[:, 0], False)
    hb, sb = rg_lru_scan(xc, p['lru_wa'][1], p['lru_ba'][1], p['lru_wx'][1], p['lru_bx'][1], p['lru_L'][1], lru_h0[:, 1], True)
    y_a = ((hf + hb) * jax.nn.gelu(zg)) @ p['lru_up']

    q = q.reshape(B, T, GLA_HEADS, GLA_DK)
    k = k.reshape(B, T, GLA_HEADS, GLA_DK)
    v = v.reshape(B, T, GLA_HEADS, GLA_DV)
    la_f = (jax.nn.log_sigmoid(lr[..., :GLA_RANK] @ p['gla_w2'][0] + p['gla_b2'][0]) / GLA_TAU).reshape(B, T, GLA_HEADS, GLA_DK)
    la_b = (jax.nn.log_sigmoid(lr[..., GLA_RANK:] @ p['gla_w2'][1] + p['gla_b2'][1]) / GLA_TAU).reshape(B, T, GLA_HEADS, GLA_DK)
    if rows is not None:
        q, k, v, la_f, la_b = (to_col_major(t, rows) for t in (q, k, v, la_f, la_b))
    o_f, gf = gla_chunked(q, k, v, la_f, gla_s0[:, 0])
    o_b, gbs = gla_chunked(jnp.flip(q, 1), jnp.flip(k, 1), jnp.flip(v, 1), jnp.flip(la_b, 1), gla_s0[:, 1])
    o = o_f + jnp.flip(o_b, 1)
    if rows is not None:
        o = from_col_major(o, rows)
    o = rmsnorm(o, p['gla_norm_g']).reshape(B, T, GLA_DV_TOT) * jax.nn.silu(g)
    y_b = o @ p['gla_up']

    m = (jax.nn.sigmoid(ga) * y_a + jax.nn.sigmoid(gb) * y_b) @ p['w_out']
    return m, jnp.stack([sf, sb], axis=1), jnp.stack([gf, gbs], axis=1)


def layer(x, mod, p, lru_h0, gla_s0, rows):
    sh1, sc1, g1, sh2, sc2, g2 = jnp.split(mod, N_MOD, axis=-1)
    ng = p['norm_g']
    h = rmsnorm(x, ng[0]) * (1.0 + sc1) + sh1
    m, s_lru, s_gla = token_mix(h, p, lru_h0, gla_s0, rows)
    x = x + g1 * rmsnorm(m, ng[1])
    h = rmsnorm(x, ng[2]) * (1.0 + sc2) + sh2
    f = jnp.square(jax.nn.relu(h @ p['mlp_w1'])) @ p['mlp_w2']
    x = x + g2 * rmsnorm(f, ng[3])
    return x, s_lru, s_gla


def setup_inputs(seed: int = 0) -> dict:
    key = jax.random.key(seed)
    ks = jax.random.split(key, 26)
    nrm = jax.random.normal
    D = D_MODEL
    u = jax.random.uniform(ks[15], (DEPTH, 2, D_RNN), minval=0.9, maxval=0.999)
    return {
        'x_prompt': nrm(ks[0], (BATCH, SEQ, D), jnp.float32),
        'x_sample': nrm(ks[1], (DEC_BATCH, DEC_SEQ, D), jnp.float32),
        'state_lru': 0.5 * nrm(ks[2], (DEC_BATCH, DEPTH, 2, D_RNN), jnp.float32),
        'state_gla': nrm(ks[3], (DEC_BATCH, DEPTH, 2, GLA_HEADS, GLA_DK, GLA_DV), jnp.float32),
        'c': nrm(ks[4], (DEC_BATCH, D), jnp.float32),
        'c_ctx': nrm(ks[5], (D,), jnp.float32),
        'w_mod': 0.5 * D ** -0.5 * nrm(ks[6], (DEPTH, D, N_MOD * D), jnp.float32),
        'b_mod': 0.02 * nrm(ks[7], (DEPTH, N_MOD * D), jnp.float32),
        'norm_g': 1.0 + 0.02 * nrm(ks[8], (DEPTH, 4, D), jnp.float32),
        'w_in': D ** -0.5 * nrm(ks[9], (DEPTH, D, IN_TOTAL), jnp.float32),
        'conv_w': CONV_W ** -0.5 * nrm(ks[10], (DEPTH, CONV_W, D_RNN), jnp.float32),
        'conv_b': 0.02 * nrm(ks[11], (DEPTH, D_RNN), jnp.float32),
        'lru_wa': LRU_BW ** -0.5 * nrm(ks[12], (DEPTH, 2, LRU_BLOCKS, LRU_BW, LRU_BW), jnp.float32),
        'lru_ba': 0.02 * nrm(ks[13], (DEPTH, 2, D_RNN), jnp.float32),
        'lru_wx': LRU_BW ** -0.5 * nrm(ks[14], (DEPTH, 2, LRU_BLOCKS, LRU_BW, LRU_BW), jnp.float32),
        'lru_bx': 0.02 * nrm(ks[16], (DEPTH, 2, D_RNN), jnp.float32),
        'lru_L': jnp.log(u) - jnp.log1p(-u),
        'lru_up': D_RNN ** -0.5 * nrm(ks[17], (DEPTH, D_RNN, D), jnp.float32),
        'gla_w2': GLA_RANK ** -0.5 * nrm(ks[18], (DEPTH, 2, GLA_RANK, GLA_DK_TOT), jnp.float32),
        'gla_b2': 0.02 * nrm(ks[19], (DEPTH, 2, GLA_DK_TOT), jnp.float32),
        'gla_norm_g': 1.0 + 0.02 * nrm(ks[20], (DEPTH, GLA_DV), jnp.float32),
        'gla_up': GLA_DV_TOT ** -0.5 * nrm(ks[21], (DEPTH, GLA_DV_TOT, D), jnp.float32),
        'w_out': D ** -0.5 * nrm(ks[22], (DEPTH, D, D), jnp.float32),
        'mlp_w1': D ** -0.5 * nrm(ks[23], (DEPTH, D, D_FF), jnp.float32),
        'mlp_w2': D_FF ** -0.5 * nrm(ks[24], (DEPTH, D_FF, D), jnp.float32),
    }


def reference(x_prompt, x_sample, state_lru, state_gla, c, c_ctx, w_mod, b_mod, norm_g, w_in, conv_w, conv_b,
              lru_wa, lru_ba, lru_wx, lru_bx, lru_L, lru_up, gla_w2, gla_b2, gla_norm_g, gla_up, w_out, mlp_w1, mlp_w2):
    rows = x_sample.shape[1] // GRID_W
    B0 = x_prompt.shape[0]
    xp = x_prompt
    xs = x_sample
    lru_states = []
    gla_states = []
    for l in range(DEPTH):
        p = {
            'norm_g': norm_g[l], 'w_in': w_in[l], 'conv_w': conv_w[l], 'conv_b': conv_b[l],
            'lru_wa': lru_wa[l], 'lru_ba': lru_ba[l], 'lru_wx': lru_wx[l], 'lru_bx': lru_bx[l], 'lru_L': lru_L[l],
            'lru_up': lru_up[l], 'gla_w2': gla_w2[l], 'gla_b2': gla_b2[l], 'gla_norm_g': gla_norm_g[l],
            'gla_up': gla_up[l], 'w_out': w_out[l], 'mlp_w1': mlp_w1[l], 'mlp_w2': mlp_w2[l],
        }
        mod_ctx = (jax.nn.silu(c_ctx) @ w_mod[l] + b_mod[l])[None, None, :]
        mod_lat = (jax.nn.silu(c) @ w_mod[l] + b_mod[l])[:, None, :]
        h0_lru = jnp.zeros((B0, 2, D_RNN), xp.dtype)
        h0_gla = jnp.zeros((B0, 2, GLA_HEADS, GLA_DK, GLA_DV), xp.dtype)
        xp, s_lru, s_gla = layer(xp, mod_ctx, p, h0_lru, h0_gla, None)
        lru_states.append(s_lru)
        gla_states.append(s_gla)
        xs, _, _ = layer(xs, mod_lat, p, state_lru[:, l], state_gla[:, l], rows)
    new_state_lru = jnp.stack(lru_states, axis=1)
    new_state_gla = jnp.stack(gla_states, axis=1)
    return (xp, xs, new_state_lru, new_state_gla)
```

```python
import math
from contextlib import ExitStack

import numpy as np
import concourse.bass as bass
import concourse.mybir as mybir
from concourse.bass_utils import run_bass_kernel_spmd

F32 = mybir.dt.float32
BF16 = mybir.dt.bfloat16
AF = mybir.ActivationFunctionType
ALU = mybir.AluOpType

COMPUTE = ("pe", "act", "dve", "pool")

D = 1024
DFF = 4096
NH = 4
DK = 128
DV = 256
EPS = 1e-6
OFF = dict(zx=0, zg=1024, q=2048, k=2560, v=3072, g=4096, lr=5120, ga=5152, gb=6176)
ARENA_W = 51900
NCORES = 8
DEBUG = False


class Op:
    __slots__ = ("eng", "fn", "reads", "writes", "dma", "semkey", "deps", "sig", "sigval", "name", "ninstr")

    def __init__(self, eng, fn, reads, writes, dma, semkey, name):
        self.eng = eng
        self.fn = fn
        self.reads = reads
        self.writes = writes
        self.dma = dma
        self.semkey = semkey
        self.deps = []
        self.sig = False
        self.sigval = None
        self.name = name


class Sched:
    def __init__(self, nc):
        self.nc = nc
        self.ops = {e: [] for e in ("pe", "act", "dve", "pool", "sp")}
        self.last_w = {}
        self.readers = {}
        self.bar_deps = {}
        self.all_ops = []
        self.last_dma = {}

    def add(self, eng, fn, reads=(), writes=(), dma=0, semkey=None, name=""):
        op = Op(eng, fn, tuple(reads), tuple(writes), dma, semkey, name)
        if dma:
            assert semkey is not None
        deps = {}
        for r in op.reads:
            y = self.last_w.get(r)
            if y is not None:
                deps[id(y)] = (y, True)
        for w in op.writes:
            y = self.last_w.get(w)
            if y is not None and id(y) not in deps:
                deps[id(y)] = (y, False)
            for y in self.readers.get(w, ()):
                if id(y) not in deps:
                    deps[id(y)] = (y, False)
        if eng in self.bar_deps:
            for y in self.bar_deps.pop(eng):
                deps[id(y)] = (y, True)
        for r in op.reads:
            self.readers.setdefault(r, []).append(op)
        for w in op.writes:
            self.last_w[w] = op
            self.readers[w] = []
        for y, raw in deps.values():
            if y is op:
                continue
            if (not y.dma) and (not op.dma) and y.eng == eng and eng == "pe" and not raw:
                continue
            op.deps.append(y)
            y.sig = True
        self.ops[eng].append(op)
        self.all_ops.append(op)
        if dma:
            self.last_dma[semkey] = op
        return op

    def barrier(self):
        deps = []
        for e in self.ops:
            for op in reversed(self.ops[e]):
                if not op.dma:
                    deps.append(op)
                    break
        for k, op in self.last_dma.items():
            deps.append(op)
        for e in self.ops:
            self.bar_deps[e] = list(deps)

    def emit(self, final_eng="sp"):
        nc = self.nc
        self.barrier()
        self.add(final_eng, None, name="final")
        for e, lst in self.ops.items():
            cnt = 0
            for op in lst:
                if (not op.dma) and op.sig:
                    cnt += 1
                    op.sigval = cnt
        dma_keys = {}
        for op in self.all_ops:
            if op.dma:
                dma_keys[op.semkey] = dma_keys.get(op.semkey, 0) + 16 * op.dma
                op.sigval = dma_keys[op.semkey]
        sems = {}
        es = ExitStack()
        for e in COMPUTE:
            sems[("eng", e)] = es.enter_context(nc.semaphore("s_" + e))
        for i, k in enumerate(dma_keys):
            sems[("dma", k)] = es.enter_context(nc.semaphore("d%d" % i))
        self.nsems = len(sems)
        stats = {}
        with nc.Block() as block:
            def body(ename):
                def run(eng):
                    waited = {}
                    nw = 0
                    for op in self.ops[ename]:
                        for y in op.deps:
                            key = ("dma", y.semkey) if y.dma else ("eng", y.eng)
                            if waited.get(key, 0) >= y.sigval:
                                continue
                            waited[key] = y.sigval
                            eng.wait_ge(sems[key], y.sigval)
                            nw += 1
                        if op.fn is None:
                            continue
                        if op.dma:
                            op.fn(eng, sems[("dma", op.semkey)])
                        else:
                            ins = op.fn(eng)
                            if op.sig:
                                ins.then_inc(sems[("eng", ename)], 1)
                    stats[ename] = (len(self.ops[ename]), nw)
                return run
            block.tensor(body("pe"))
            block.scalar(body("act"))
            block.vector(body("dve"))
            block.gpsimd(body("pool"))
            block.sync(body("sp"))
        es.close()
        return stats


class Arena:
    def __init__(self, ap, n):
        self.ap = ap
        self.n = n
        self.off = 0
        self.peak = 0

    def f32(self, words, parts=128):
        off = self.off
        self.off += words
        self.peak = max(self.peak, self.off)
        assert self.off <= self.n, ("arena overflow", self.off, self.n)
        return self.ap[0:parts, off:off + words]

    def bf(self, elems, parts=128):
        return self.f32((elems + 1) // 2, parts).bitcast(BF16)

    def mark(self):
        return self.off

    def release(self, m):
        self.off = m


def rev(ap):
    a = ap.ap
    assert len(a) == 2
    return bass.AP(ap.tensor, ap.offset + (a[1][1] - 1) * a[1][0], [list(a[0]), [-a[1][0], a[1][1]]])


def bcast_mid(ap, n):
    a = ap.ap
    assert len(a) == 2
    return bass.AP(ap.tensor, ap.offset, [list(a[0]), [0, n], list(a[1])])


class Job:
    pass


class Builder:
    def __init__(self, nc):
        self.nc = nc
        self.S = Sched(nc)
        self.es = ExitStack()
        arena_t = self.es.enter_context(nc.sbuf_tensor("arena", [128, ARENA_W], F32))
        self.A = Arena(arena_t, ARENA_W)
        self.psum = self.es.enter_context(nc.psum_tensor("psum", [128, 8 * 512], F32))
        self.wname = {}
        self.dbg = {}

    def PS(self, b, n=1):
        return self.psum[:, b * 512:(b + n) * 512]

    def PSbf(self, b, nb, c):
        return self.psum[:, b * 512:(b + nb) * 512].bitcast(BF16).rearrange("p (c t) -> p c t", c=c)

    def act(self, out, in_, func, r, w, scale=None, bias=None, accum=None, name=""):
        kw = {}
        if scale is not None:
            kw["scale"] = scale
        if bias is not None:
            kw["bias"] = bias
        if accum is not None:
            kw["accum_out"] = accum
        self.S.add("act", lambda e: e.activation(out=out, in_=in_, func=func, **kw), r, w, name=name)

    def tt(self, eng, out, a, b, op, r, w, name=""):
        self.S.add(eng, lambda e: e.tensor_tensor(out=out, in0=a, in1=b, op=op), r, w, name=name)

    def stt(self, out, in0, scalar, in1, op0, op1, r, w, name=""):
        self.S.add("dve", lambda e: e.scalar_tensor_tensor(out=out, in0=in0, scalar=scalar, in1=in1, op0=op0, op1=op1),
                   r, w, name=name)

    def ts(self, eng, out, in0, s1, s2, op0, op1, r, w, name=""):
        if op1 is None:
            self.S.add(eng, lambda e: e.tensor_scalar(out=out, in0=in0, scalar1=s1, scalar2=None, op0=op0), r, w, name=name)
        else:
            self.S.add(eng, lambda e: e.tensor_scalar(out=out, in0=in0, scalar1=s1, scalar2=s2, op0=op0, op1=op1), r, w, name=name)

    def copy(self, eng, out, in_, r, w, name=""):
        if eng == "act":
            self.act(out, in_, AF.Copy, r, w, name=name)
        else:
            self.S.add(eng, lambda e: e.tensor_copy(out=out, in_=in_), r, w, name=name)

    def memset(self, eng, ap, val, w):
        self.S.add(eng, lambda e: e.memset(ap, val), (), w)

    def recip(self, out, in_, r, w):
        self.S.add("dve", lambda e: e.reciprocal(out=out, in_=in_), r, w)

    def mms(self, lst, r, w, name=""):
        def fn(e):
            ins = None
            for (o, l, rr, st, sp) in lst:
                ins = e.matmul(o, lhsT=l, rhs=rr, start=st, stop=sp)
            return ins
        op = self.S.add("pe", fn, r, w, name=name)
        op.ninstr = len(lst)

    def transposes(self, lst, r, w):
        ident = self.ident

        def fn(e):
            ins = None
            for (o, i) in lst:
                ins = e.transpose(out=o, in_=i, identity=ident)
            return ins
        op = self.S.add("pe", fn, tuple(r) + ("ident",), w, name="T")
        op.ninstr = len(lst)

    def dma(self, q, pairs, r, w, key):
        def fn(e, sem):
            for (o, i) in pairs:
                e.dma_start(out=o, in_=i).then_inc(sem, 16)
        self.S.add(q, fn, r, w, dma=len(pairs), semkey=key)

    def load_w(self, slot, name, src, ncols=512):
        if self.wname.get(slot) == name:
            return self.wslot[slot]
        self.wname[slot] = name
        dst = self.wslot[slot][:, :, 0:ncols]
        self.dma("pool", [(dst, src.rearrange("(c p) n -> p c n", p=128))], (), [("w", slot)], ("w", slot))
        return self.wslot[slot]

    def build(self, T):
        nc, S, A = self.nc, self.S, self.A
        self.T = T
        self.ident = A.bf(128)
        onesf = A.f32(128)
        self.memset("pool", onesf, 1.0, ["onesf"])
        S.add("pool", lambda e: e.affine_select(out=self.ident, in_=onesf, pattern=[[1, 128]], compare_op=ALU.is_equal,
                                                fill=0.0, base=0, channel_multiplier=-1), ["onesf"], ["ident"])
        self.identf = A.f32(128)
        S.add("pool", lambda e: e.affine_select(out=self.identf, in_=onesf, pattern=[[1, 128]], compare_op=ALU.is_equal,
                                                fill=0.0, base=0, channel_multiplier=-1), ["onesf"], ["identf"])
        cval = A.f32(128)
        self.memset("pool", cval, -1.0 / 16.0, ["cval"])
        self.negcol = cval[:, 0:1]
        self.Mc = [A.f32(128), A.f32(128)]
        self.Mt = [A.f32(128), A.f32(128)]
        self.SM = [A.f32(128), A.f32(128)]
        specs = [(self.Mc[0], cval, 1, -1, ALU.is_ge), (self.Mc[1], cval, -1, 1, ALU.is_ge),
                 (self.Mt[0], cval, -1, 1, ALU.is_gt), (self.Mt[1], cval, 1, -1, ALU.is_gt),
                 (self.SM[0], onesf, 1, -1, ALU.is_ge), (self.SM[1], onesf, -1, 1, ALU.is_ge)]
        for (o, i, st, cm, cmp) in specs:
            S.add("pool", lambda e, o=o, i=i, st=st, cm=cm, cmp=cmp: e.affine_select(
                out=o, in_=i, pattern=[[st, 128]], compare_op=cmp, fill=0.0, base=0, channel_multiplier=cm),
                ["onesf", "cval"], ["masks"])
        self.cw = A.f32(32).rearrange("p (c k) -> p c k", k=4)
        self.cb = A.f32(8)
        self.lba = A.f32(16).rearrange("p (d c) -> p d c", d=2)
        self.lbx = A.f32(16).rearrange("p (d c) -> p d c", d=2)
        self.lL = A.f32(16).rearrange("p (d c) -> p d c", d=2)
        self.cL = A.f32(16).rearrange("p (d c) -> p d c", d=2)
        self.h0 = A.f32(16).rearrange("p (d c) -> p d c", d=2)
        self.gnb = A.f32(256)
        pl = [(self.cw[:, :, k], T["conv_w"][k].rearrange("(c p) -> p c", p=128)) for k in range(4)]
        pl.append((self.cb, T["conv_b"].rearrange("(c p) -> p c", p=128)))
        for (dst, nm) in ((self.lba, "lru_ba"), (self.lbx, "lru_bx"), (self.lL, "lru_L"), (self.h0, "st_lru")):
            for d in range(2):
                pl.append((dst[:, d, :], T[nm][d].rearrange("(c p) -> p c", p=128)))
        pl.append((self.gnb, T["gla_norm_g"].partition_broadcast(128)))
        self.dma("sp", pl, (), ["params"], "params")
        tmp16 = A.f32(16).rearrange("p (d c) -> p d c", d=2)
        self.act(tmp16, self.lL, AF.Exp, ["params"], ["tmp16"], scale=-1.0)
        self.act(tmp16, tmp16, AF.Ln, ["tmp16"], ["tmp16b"], bias=1.0)
        self.ts("dve", self.cL, tmp16, -8.0, None, ALU.mult, None, ["tmp16b"], ["cL"])
        self.hcL = A.f32(16).rearrange("p (d c) -> p d c", d=2)
        self.hba = A.f32(16).rearrange("p (d c) -> p d c", d=2)
        self.hbx = A.f32(16).rearrange("p (d c) -> p d c", d=2)
        self.ts("dve", self.hcL, self.cL, 0.5, None, ALU.mult, None, ["cL"], ["hpar"])
        self.ts("dve", self.hba, self.lba, 0.5, None, ALU.mult, None, ["params", "hpar"], ["hpar"])
        self.ts("dve", self.hbx, self.lbx, 0.5, None, ALU.mult, None, ["params", "hpar"], ["hpar"])
        self.w2aug = []
        for d in range(2):
            t = A.bf(512)
            self.w2aug.append(t)
            self.dma("pool", [(t[0:16, :], T["gla_w2"][d]), (t[16:17, :], T["gla_b2"][d:d + 1, :])], (), [("w2aug", d)], ("w2aug", d))
        self.NW = 4
        wraw = A.f32(self.NW * 2048)
        self.wslot = [wraw[:, i * 2048:(i + 1) * 2048].bitcast(BF16).rearrange("p (c n) -> p c n", c=8) for i in range(self.NW)]
        self.wpair = [wraw[:, i * 4096:(i + 1) * 4096].rearrange("p (c n) -> p c n", c=8) for i in range(self.NW // 2)]
        self.wlr = A.bf(8 * 32).rearrange("p (c n) -> p c n", c=8)
        self.dma("pool", [(self.wlr, T["w_in"][:, OFF["lr"]:OFF["lr"] + 32].rearrange("(c p) n -> p c n", p=128))],
                 (), ["wlr"], "wlr")
        self.BD = {}
        for wi, wname in enumerate(("lru_wa", "lru_wx")):
            for d in range(2):
                t = A.bf(8 * 128).rearrange("p (c j) -> p c j", c=8)
                self.BD[(wi, d)] = t
                key = ("BD", wi, d)
                self.memset("pool", t, 0.0, [key])
                v = T[wname][d].rearrange("(c two) i j -> two i c j", two=2)
                self.dma("pool", [(t[0:64, :, 0:64], v[0]), (t[64:128, :, 64:128], v[1])], (), [key], key)
        self.setup_mod()
        base_mark = A.mark()
        mt = self.mod_alloc()
        for cg in range(4):
            self.mod_tile(mt, cg, cg)
        self.mod_pending = list(range(4, 12))
        S.barrier()
        A.release(base_mark)
        jP = Job()
        jP.name, jP.NT, jP.seqs, jP.convL, jP.perm = "P", 512, [(0, 256), (256, 256)], 256, False
        jP.x, jP.y, jP.mrow, jP.state_in, jP.state_out = T["xp"], T["yp"], 0, False, True
        jP.YAd, jP.OGd = T["YAd_P"], T["OGd_P"]
        jS = Job()
        jS.name, jS.NT, jS.seqs, jS.convL, jS.perm = "S", 2048, [(0, 2048)], 64, True
        jS.x, jS.y, jS.mrow, jS.state_in, jS.state_out = T["xs"], T["ys"], 1, True, False
        jS.YAd, jS.OGd = T["YAd_S"], T["OGd_S"]
        self.conv_pending = [(k, sp[1]) for k, sp in enumerate(self.f_specs())]
        self.Hbuf = A.bf(8 * 2048).rearrange("p (c t) -> p c t", c=8)
        mark1 = A.mark()
        for job in (jP, jS):
            self.phase_S1(job, self.Hbuf, perm=job.perm, with_mod=(job is jS))
            S.barrier()
            A.release(mark1)

            self.phase_G(job)
            S.barrier()
            A.release(mark1)
            if job.perm:
                self.phase_S1(job, self.Hbuf, perm=False)
                S.barrier()
                A.release(mark1)
            self.phase_L(job)
            S.barrier()
            A.release(mark1)
        self.convert_some(100)
        assert not self.mod_pending
        S.barrier()
        A.release(base_mark)
        self.phase_F([jS, jP])
        stats = S.emit()
        self.es.close()
        return stats

    def setup_mod(self):
        A, T = self.A, self.T
        cvT = A.f32(16).rearrange("p (c r) -> p c r", r=2)
        self.scT = A.bf(16).rearrange("p (c r) -> p c r", r=2)
        self.dma("sp", [(cvT[:, :, r], T["cvec"][r].rearrange("(c p) -> p c", p=128)) for r in range(2)], (), ["cvT"], "cvT")
        self.act(self.scT, cvT, AF.Silu, ["cvT"], ["scT"])

    def mod_alloc(self):
        A = self.A
        return [dict(bm=A.f32(512, parts=2), ng=A.f32(512, parts=2), rows=A.f32(512, parts=2)) for _ in range(2)]

    def mod_tile(self, mt, cg, bank, xslot=None):
        T = self.T
        t = mt[cg % 2]
        k = cg % 2
        seg, hs = cg // 2, (cg % 2) * 512
        cs = slice(cg * 512, (cg + 1) * 512)
        ngi = {1: 0, 2: 1, 4: 2, 5: 3}.get(seg)
        pairs_ = [(t["bm"], T["b_mod"][cs].partition_broadcast(2))]
        if ngi is not None:
            pairs_.append((t["ng"], T["norm_g"][ngi][hs:hs + 512].partition_broadcast(2)))
        self.dma("sp", pairs_, (), [("mbm", k)], ("mbm", k))
        if xslot is None:
            sl = cg % self.NW
            wt = self.load_w(sl, ("wmod", cg), T["w_mod"][:, cs])
            wk = [("w", sl)]
        else:
            wt = xslot
            wk = ["wx"]
            self.dma("pool", [(wt, T["w_mod"][:, cs].rearrange("(c p) n -> p c n", p=128))], (), wk, "wx")
        ps = self.PS(bank)[0:2, :]
        self.mms([(ps, self.scT[:, kc, :], wt[:, kc, :], kc == 0, kc == 7) for kc in range(8)],
                 ["scT"] + wk, [("ps", bank)])
        self.tt("dve", t["rows"], ps, t["bm"], ALU.add, [("ps", bank), ("mbm", k)], [("mrows", k)])
        if seg in (1, 4):
            self.stt(t["rows"], t["rows"], 1.0, t["ng"], ALU.add, ALU.mult, [("mrows", k), ("mbm", k)], [("mrows", k)])
        elif seg in (2, 5):
            self.tt("dve", t["rows"], t["rows"], t["ng"], ALU.mult, [("mrows", k), ("mbm", k)], [("mrows", k)])
        self.dma("sp", [(T["modd"][:, cs], t["rows"])], [("mrows", k)], [("modd", cg)], ("mrows", k))

    def build_mod(self, job, which, ngt=None):
        T = self.T
        r = job.mrow

        def seg(i):
            return T["modd"][r, i * 1024:(i + 1) * 1024].partition_broadcast(128)

        def mk(*segs):
            return [("modd", 2 * i + h) for i in segs for h in range(2)]
        if 0 in which:
            sc1, b1 = self.modA
            self.dma("sp", [(sc1, seg(1)), (b1, seg(0))], mk(0, 1), ["modA"], "modA")
        if 1 in which:
            g1n, sc2, b2, g2n = self.modB
            self.dma("sp", [(g1n, seg(2)), (sc2, seg(4)), (b2, seg(3)), (g2n, seg(5))], mk(2, 3, 4, 5), ["modB"], "modB")

    def tile_rows(self, job, dram, i, perm):
        if not perm:
            return [(0, 128, dram[i * 128:(i + 1) * 128, :])]
        v = dram.rearrange("(r w) d -> w r d", w=64)
        return [(32 * k, 32 * k + 32, v[4 * i + k]) for k in range(4)]

    def s1_alloc(self):
        A = self.A
        st = {}
        st["xt"] = [A.f32(1024) for _ in range(8)]
        st["t"] = [A.f32(1024) for _ in range(4)]
        st["hb"] = [A.bf(1024) for _ in range(8)]
        st["junk"] = A.bf(1024)
        st["stat"] = A.f32(24)
        return st

    def s1_pre(self, st, job, gi, perm):
        sc1, b1 = self.modA
        stat = st["stat"]
        xo = (gi % 2) * 4
        xts = st["xt"][xo:xo + 4]
        hbs = st["hb"][xo:xo + 4]
        so = (gi % 2) * 12
        for j in range(4):
            rows = self.tile_rows(job, job.x, gi * 4 + j, perm)
            self.dma("sp", [(xts[j][lo:hi, :], src) for (lo, hi, src) in rows], (), [("xt", xo + j)], ("xt", xo + j))
        for j in range(4):
            self.act(st["junk"], xts[j], AF.Square, [("xt", xo + j)], ["junk", ("ss", xo + j)], accum=stat[:, so + j:so + j + 1])
        for j in range(4):
            self.act(stat[:, so + 4 + j:so + 5 + j], stat[:, so + j:so + j + 1], AF.Ln, [("ss", xo + j)], [("sq", xo + j)],
                     scale=1.0 / D, bias=EPS)
            self.act(stat[:, so + 8 + j:so + 9 + j], stat[:, so + 4 + j:so + 5 + j], AF.Exp, [("sq", xo + j)], [("rs", xo + j)], scale=-0.5)
        for j in range(4):
            self.stt(st["t"][j], xts[j], stat[:, so + 8 + j:so + 9 + j], sc1, ALU.mult, ALU.mult,
                     [("xt", xo + j), ("rs", xo + j), "modA"], [("s1t", j)])
        for j in range(4):
            self.tt(("pool", "dve")[j % 2], hbs[j], st["t"][j], b1, ALU.add, [("s1t", j), "modA"], [("hb", xo + j)])

    def s1_post(self, st, gi, Hdst, hkey):
        pbank = (gi % 2) * 4
        xo = (gi % 2) * 4
        hbs = st["hb"][xo:xo + 4]
        pT = self.PSbf(pbank, 4, 8)
        banks = [("ps", pbank + k) for k in range(4)]
        for j in range(4):
            self.transposes([(pT[:, c, j * 128:(j + 1) * 128], hbs[j][:, c * 128:(c + 1) * 128]) for c in range(8)],
                            [("hb", xo + j)], banks)
        self.copy("act", Hdst[:, 0:4, :], pT[:, 0:4, :], banks, [(hkey, 0)])
        self.copy("dve", Hdst[:, 4:8, :], pT[:, 4:8, :], banks, [(hkey, 1)])

    def phase_S1(self, job, H, perm, with_mod=False):
        self.modA = [self.A.f32(1024), self.A.f32(1024)]
        self.build_mod(job, (0,))
        st = self.s1_alloc()
        NG = job.NT // 512
        self.s1_pre(st, job, 0, perm)
        for gi in range(NG):
            if gi + 1 < NG:
                self.s1_pre(st, job, gi + 1, perm)
            self.s1_post(st, gi, H[:, :, gi * 512:(gi + 1) * 512], ("H", gi))

    def phase_L(self, job):
        A, T, S = self.A, self.T, self.S
        NT = job.NT
        TT = NT // 512
        H = self.Hbuf
        zx = [A.f32(NT), A.f32(NT)]
        xc = [A.f32(NT), A.f32(NT)]
        xcb = [A.bf(NT), A.bf(NT)]
        gz = [A.bf(NT), A.bf(NT)]
        a_ = [A.f32(NT), A.f32(NT)]
        iu = [A.f32(NT), A.f32(NT)]
        s_ = [A.f32(NT), A.f32(NT)]
        ya = [A.bf(NT), A.bf(NT)]
        nseq = len(job.seqs)
        lruo = A.f32(nseq * 16).rearrange("p (s d c) -> p s d c", s=nseq, d=2)
        L = job.convL
        LN_HALF = math.log(0.5)

        def seg(ap, lo, hi):
            return ap.rearrange("p (s l) -> p s l", l=L)[:, :, lo:hi]
        Hk = [[(("H", tt), 0), (("H", tt), 1)] for tt in range(TT)]
        wts = {}
        for blk in range(2):
            wts[("x", blk)] = self.load_w(blk * 2, ("w_in", "zx", blk), T["w_in"][:, OFF["zx"] + blk * 512: OFF["zx"] + (blk + 1) * 512])
            wts[("g", blk)] = self.load_w(blk * 2 + 1, ("w_in", "zg", blk), T["w_in"][:, OFF["zg"] + blk * 512: OFF["zg"] + (blk + 1) * 512])

        def mm_zx(c, tt):
            blk, cbk = c // 4, c % 4
            wx = wts[("x", blk)]
            ts_ = slice(tt * 512, (tt + 1) * 512)
            b = tt % 2
            self.mms([(self.PS(b), wx[:, kc, cbk * 128:(cbk + 1) * 128], H[:, kc, ts_], kc == 0, kc == 7) for kc in range(8)],
                     [("w", blk * 2)] + Hk[tt], [("ps", b)])

        def mm_zg(c, tt):
            blk, cbk = c // 4, c % 4
            wg = wts[("g", blk)]
            ts_ = slice(tt * 512, (tt + 1) * 512)
            b = 2 + tt % 2
            self.mms([(self.PS(b), wg[:, kc, cbk * 128:(cbk + 1) * 128], H[:, kc, ts_], kc == 0, kc == 7) for kc in range(8)],
                     [("w", blk * 2 + 1)] + Hk[tt], [("ps", b)])

        NHEAD = min(2, TT)

        def stage_A_head(c):
            for tt in range(NHEAD):
                mm_zx(c, tt)
            for tt in range(NHEAD):
                mm_zg(c, tt)

        def stage_A1(c):
            p = c % 2
            for tt in range(TT):
                ts_ = slice(tt * 512, (tt + 1) * 512)
                if tt >= NHEAD:
                    mm_zx(c, tt)
                self.copy("dve", zx[p][:, ts_], self.PS(tt % 2), [("ps", tt % 2)], [("zx", p, tt)])
            zxk = [("zx", p, tt) for tt in range(TT)]
            self.ts("pool", xc[p], zx[p], self.cw[:, c, 1:2], self.cb[:, c:c + 1], ALU.mult, ALU.add, zxk + ["params"], [("xc", p)])

        def stage_A1b(c):
            p = c % 2
            zxk = [("zx", p, tt) for tt in range(TT)]
            for (k, lo_o, hi_o, lo_i, hi_i) in ((0, 1, L, 0, L - 1), (2, 0, L - 1, 1, L), (3, 0, L - 2, 2, L)):
                self.stt(seg(xc[p], lo_o, hi_o), seg(zx[p], lo_i, hi_i), self.cw[:, c, k:k + 1], seg(xc[p], lo_o, hi_o),
                         ALU.mult, ALU.add, zxk + [("xc", p), "params"], [("xc", p)])
            self.copy("dve", xcb[p], xc[p], [("xc", p)], [("xcb", p)])

        def stage_A2(c):
            p = c % 2
            for tt in range(TT):
                ts_ = slice(tt * 512, (tt + 1) * 512)
                if tt >= NHEAD:
                    mm_zg(c, tt)
                self.act(gz[p][:, ts_], self.PS(2 + tt % 2), AF.Gelu_apprx_tanh, [("ps", 2 + tt % 2)], [("gz", p, tt)])

        def stage_B(c):
            p = c % 2
            for tt in range(TT):
                ts_ = slice(tt * 512, (tt + 1) * 512)
                for d in range(2):
                    br, bi = 4 + 2 * d, 5 + 2 * d
                    self.mms([(self.PS(br), self.BD[(0, d)][:, c, :], xcb[p][:, ts_], True, True)],
                             [("BD", 0, d), ("xcb", p)], [("ps", br)])
                    self.mms([(self.PS(bi), self.BD[(1, d)][:, c, :], xcb[p][:, ts_], True, True)],
                             [("BD", 1, d), ("xcb", p)], [("ps", bi)])
                for d in range(2):
                    br, bi = 4 + 2 * d, 5 + 2 * d
                    self.act(a_[d][:, ts_], self.PS(br), AF.Tanh, [("ps", br), "hpar"], [("a", d)],
                             scale=0.5, bias=self.hba[:, d, c:c + 1])
                    self.act(iu[d][:, ts_], self.PS(bi), AF.Tanh, [("ps", bi), "hpar"], [("iu", d)],
                             scale=0.5, bias=self.hbx[:, d, c:c + 1])

        def stage_C(c, part):
            p = c % 2
            if part == 0:
                stage_C0(c)
            else:
                stage_C1(c)

        def stage_Cpre(c):
            p = c % 2
            for d in range(2):
                self.stt(iu[d], iu[d], 1.0, xc[p], ALU.add, ALU.mult, [("iu", d), ("xc", p)], [("iu", d)])

        def stage_C0(c):
            p = c % 2
            for d in range(2):
                self.act(a_[d], a_[d], AF.Exp, [("a", d), "hpar"], [("a", d)],
                         scale=self.hcL[:, d, c:c + 1], bias=self.hcL[:, d, c:c + 1])
                self.act(s_[d], a_[d], AF.Square, [("a", d)], [("s", d)])
                self.act(s_[d], s_[d], AF.Ln, [("s", d)], [("s", d)], scale=-1.0, bias=1.0)
                self.act(s_[d], s_[d], AF.Exp, [("s", d)], [("s", d)], scale=0.5, bias=LN_HALF)

        def stage_C1(c, dirs=(0, 1)):
            p = c % 2
            for d in dirs:
                self.tt(("dve", "pool")[d], iu[d], iu[d], s_[d], ALU.mult, [("iu", d), ("s", d)], [("iu", d)])
                for si, (st0, ln) in enumerate(job.seqs):
                    sl = slice(st0, st0 + ln)
                    init = self.h0[:, d, c:c + 1] if job.state_in else 0.0
                    f = (lambda x: x) if d == 0 else rev
                    S.add("dve", lambda e, o=f(s_[d][:, sl]), d0=f(a_[d][:, sl]), d1=f(iu[d][:, sl]), init=init:
                          e.tensor_tensor_scan(out=o, data0=d0, data1=d1, initial=init, op0=ALU.mult, op1=ALU.add),
                          [("a", d), ("iu", d), "params", ("s", d)], [("s", d)])
                    if job.state_out:
                        pos = st0 + ln - 1 if d == 0 else st0
                        self.copy("pool", lruo[:, si, d, c:c + 1], s_[d][:, pos:pos + 1], [("s", d)], ["lruo"])

        def stage_C2(c):
            p = c % 2
            self.tt("pool", s_[0], s_[0], s_[1], ALU.add, [("s", 0), ("s", 1)], [("s", 0)])
            yt = ya[p]
            gzk = [("gz", p, tt) for tt in range(TT)]
            if job.perm:
                o = yt.rearrange("p (w r) -> p w r", r=32)
                i0 = s_[0].rearrange("p (r w) -> p w r", w=64)
                i1 = gz[p].rearrange("p (r w) -> p w r", w=64)
            else:
                o, i0, i1 = yt, s_[0], gz[p]
            self.tt("dve", o, i0, i1, ALU.mult, [("s", 0)] + gzk, [("ya", p)])
            self.dma("sp", [(job.YAd[c], yt)], [("ya", p)], [("YAd", job.name)], ("ya", p))

        stage_A_head(0)
        stage_A1(0)
        stage_A1b(0)
        stage_A2(0)
        stage_A_head(1)
        stage_B(0)
        for c in range(8):
            nx = c + 1 < 8
            stage_Cpre(c)
            if nx:
                stage_A1(c + 1)
            stage_C0(c)
            stage_C1(c, (0,))
            if nx:
                stage_A1b(c + 1)
            stage_C1(c, (1,))
            stage_C2(c)
            if nx:
                stage_A2(c + 1)
            if c + 2 < 8:
                stage_A_head(c + 2)
            if nx:
                stage_B(c + 1)
        if job.state_out:
            n = nseq * 16
            lflat = lruo.rearrange("p s d c -> p (s d c)")
            pst = self.PS(0)[0:n, 0:128]
            identf = self.identf
            S.add("pe", lambda e: e.transpose(out=pst, in_=lflat, identity=identf), ["lruo", "identf"], [("ps", 0)])
            lrT = A.f32(128, parts=n)
            self.copy("act", lrT, pst, [("ps", 0)], ["lrT"])
            self.dma("sp", [(T["nsl"].rearrange("s d (c p) -> (s d c) p", p=128), lrT)], ["lrT"], ["nsl"], "nsl")

    def phase_G(self, job):
        A, T, S = self.A, self.T, self.S
        NT = job.NT
        TT = NT // 512
        NTI = NT // 128
        H = self.Hbuf
        VT = A.bf(NTI * 1024).rearrange("p (i v) -> p i v", i=NTI)
        SBst = A.bf(NTI * 1024).rearrange("p (i v) -> p i v", i=NTI)
        lrA = [A.bf(NT, parts=17), A.bf(NT, parts=17)]
        Sst = [A.f32(1024), A.f32(1024)]
        Sfb = A.bf(1024)
        l_ = [A.f32(512), A.f32(512)]
        et = A.f32(512)
        ktail = A.bf(512)
        E = [A.f32(512), A.f32(512)]
        Ei = [A.f32(512), A.f32(512)]
        qe = [A.bf(512), A.bf(512)]
        ke = [A.bf(512), A.bf(512)]
        scT = [A.bf(512), A.bf(512)]
        dec = A.f32(4)
        on = A.f32(1024)
        sg = [A.bf(1024), A.bf(1024)]
        ogt = A.bf(1024)
        ogf = A.bf(1024).rearrange("p (c t) -> p c t", c=8)
        junk = A.bf(256)
        stat = A.f32(12)
        lnscale = -0.5 * math.log(DK)
        vs0 = 1
        if job.name == "P":
            self.wslot = self.wslot[:self.NW] + [A.bf(8 * 512).rearrange("p (c n) -> p c n", c=8) for _ in range(2)]
            vs0 = self.NW
        mt = xsl = None
        if job.name == "P" and self.mod_pending:
            mt = self.mod_alloc()
            xsl = A.bf(8 * 512).rearrange("p (c n) -> p c n", c=8)

        def mod_step(bank):
            if mt is not None and self.mod_pending:
                self.mod_tile(mt, self.mod_pending.pop(0), bank, xslot=xsl)
        wsrc = lambda nm, j: T["w_in"][:, OFF[nm] + j * 512: OFF[nm] + (j + 1) * 512]
        for d in range(2):
            self.memset("pool", lrA[d], 1.0, [("lrA", d)])
        for tt in range(TT):
            ts_ = slice(tt * 512, (tt + 1) * 512)
            for d in range(2):
                b = 2 * (tt % 2) + d
                ps = self.PS(b)[0:16, :]
                self.mms([(ps, self.wlr[:, kc, 16 * d:16 * d + 16], H[:, kc, ts_], kc == 0, kc == 7) for kc in range(8)],
                         ["wlr", (("H", tt), 0), (("H", tt), 1)], [("ps", b)])
                self.copy("act", lrA[d][0:16, ts_], ps, [("ps", b)], [("lrA", d)])

        def l_compute(d, i, bank):
            ps = self.PS(bank)
            self.mms([(ps, lrA[d][0:17, i * 128:(i + 1) * 128], self.w2aug[d][0:17, :], True, True)],
                     [("lrA", d), ("w2aug", d)], [("ps", bank)])
            self.act(l_[d], ps, AF.Exp, [("ps", bank)], [("l", d)], scale=-1.0)
            self.act(l_[d], l_[d], AF.Ln, [("l", d)], [("l", d)], bias=1.0)

        def tail_kv(d, i, hk, b_k, b_tail, b_dec, b_kv, with_kv=True):
            Wk = self.wslot[0]
            self.mms([(self.PS(b_k), H[:, kc, i * 128:(i + 1) * 128], Wk[:, kc, :], kc == 0, kc == 7) for kc in range(8)],
                     [("w", 0)] + hk, [("ps", b_k)])
            self.mms([(self.PS(b_tail), self.Mt[d], l_[d], True, True)], [("l", d), "masks"], [("ps", b_tail)])
            self.mms([(self.PS(b_dec)[:, h:h + 1], l_[d][:, h * 128:(h + 1) * 128], self.negcol, True, True) for h in range(NH)],
                     [("l", d), "cval"], [("ps", b_dec)])
            self.act(et, self.PS(b_tail), AF.Exp, [("ps", b_tail)], ["et"])
            self.act(dec, self.PS(b_dec)[:, 0:4], AF.Exp, [("ps", b_dec)], ["dec"])
            self.tt("dve", ktail, self.PS(b_k), et, ALU.mult, [("ps", b_k), "et"], ["ktail"])
            if with_kv:
                kv_mms(i, b_kv)

        def kv_mms(i, b_kv):
            for hp in range(2):
                self.mms([(self.PS(b_kv + hp)[:, (h % 2) * 256:(h % 2 + 1) * 256], ktail[:, h * 128:(h + 1) * 128],
                           VT[:, i, h * 256:(h + 1) * 256], True, True) for h in (2 * hp, 2 * hp + 1)],
                         ["ktail", ("VT", i, hp)], [("ps", b_kv + hp)])

        def state_update(d, b_kv):
            for h in range(NH):
                sl = slice(h * 256, (h + 1) * 256)
                bk = b_kv + h // 2
                self.stt(Sst[d][:, sl], Sst[d][:, sl], dec[:, h:h + 1], self.PS(bk)[:, (h % 2) * 256:(h % 2 + 1) * 256],
                         ALU.mult, ALU.add, [("S", d), "dec", ("ps", bk)], [("S", d)])

        for si, (st0, ln) in enumerate(job.seqs):
            i0 = st0 // 128
            nch = ln // 128
            for d in range(2):
                if job.state_in:
                    self.dma("sp", [(Sst[d].rearrange("p (h v) -> p h v", h=NH), T["st_gla"][d].rearrange("h k v -> k h v"))],
                             (), [("S", d)], ("S", d))
                else:
                    self.memset("pool", Sst[d], 0.0, [("S", d)])
            self.load_w(0, ("w_in", "k", 0), wsrc("k", 0))
            if job.name == "P":
                self.load_w(3, ("w_in", "q", 0), wsrc("q", 0))
                self.load_w(1, ("w_in", "g", 0), wsrc("g", 0))
                self.load_w(2, ("w_in", "g", 1), wsrc("g", 1))
            Wv = [self.load_w(vs0, ("w_in", "v", 0), wsrc("v", 0)), self.load_w(vs0 + 1, ("w_in", "v", 1), wsrc("v", 1))]
            for n in reversed(range(nch)):
                i = i0 + n
                hk = [(("H", i // 4), 0), (("H", i // 4), 1)]
                for j in range(2):
                    self.mms([(self.PS(1 + j), H[:, kc, i * 128:(i + 1) * 128], Wv[j][:, kc, :], kc == 0, kc == 7) for kc in range(8)],
                             [("w", vs0 + j)] + hk, [("ps", 1 + j)])
                self.copy("act", VT[:, i, 0:512], self.PS(1), [("ps", 1)], [("VT", i, 0)])
                self.copy("dve", VT[:, i, 512:1024], self.PS(2), [("ps", 2)], [("VT", i, 1)])
                l_compute(1, i, 3)
                self.copy("act", SBst[:, i, :], Sst[1], [("S", 1)], [("SBst", i)])
                if job.name == "S" and n % 4 != 3:
                    self.convert_some(1)
                tail_kv(1, i, hk, 0, 4, 7, 5)
                state_update(1, 5)
                mod_step(3)
            if job.state_out:
                self.dma("sp", [(T["nsg"][si, 1].rearrange("h k v -> k h v"), Sst[1].rearrange("p (h v) -> p h v", h=NH))],
                         [("S", 1)], ["nsg"], ("nsgo", 1))
            Wq = self.load_w(3, ("w_in", "q", 0), wsrc("q", 0))
            Wk = self.wslot[0]
            Wg = [self.load_w(1, ("w_in", "g", 0), wsrc("g", 0)), self.load_w(2, ("w_in", "g", 1), wsrc("g", 1))]
            self.copy("act", Sfb, Sst[0], [("S", 0)], ["Sfb"])
            def g2_front(n):
                i = i0 + n
                hk = [(("H", i // 4), 0), (("H", i // 4), 1)]
                tsl = slice(i * 128, (i + 1) * 128)
                l_compute(0, i, 0)
                l_compute(1, i, 1)
                for d in range(2):
                    self.mms([(self.PS(3 + d)[:, h * 128:(h + 1) * 128], l_[d][:, h * 128:(h + 1) * 128], self.Mc[d], True, True)
                              for h in range(NH)], [("l", d), "masks"], [("ps", 3 + d)])
                    self.act(E[d], self.PS(3 + d), AF.Exp, [("ps", 3 + d)], [("E", d)], bias=lnscale)
                    self.act(Ei[d], self.PS(3 + d), AF.Exp, [("ps", 3 + d)], [("Ei", d)], scale=-1.0)
                tail_kv(0, i, hk, 1, 2, 0, 1, with_kv=False)
                self.mms([(self.PS(2)[:, h * 128:(h + 1) * 128], Wq[:, kc, h * 128:(h + 1) * 128], H[:, kc, tsl], kc == 0, kc == 7)
                          for h in range(NH) for kc in range(8)], [("w", 3)] + hk, [("ps", 2)])
                self.mms([(self.PS(0)[:, h * 128:(h + 1) * 128], Wk[:, kc, h * 128:(h + 1) * 128], H[:, kc, tsl], kc == 0, kc == 7)
                          for h in range(NH) for kc in range(8)], [("w", 0)] + hk, [("ps", 0)])
                for d in range(2):
                    self.tt("dve", qe[d], self.PS(2), E[d], ALU.mult, [("ps", 2), ("E", d)], [("qe", d)])
                    self.tt("dve", ke[d], self.PS(0), Ei[d], ALU.mult, [("ps", 0), ("Ei", d)], [("ke", d)])
                kv_mms(i, 1)
                state_update(0, 1)
                for d in range(2):
                    self.mms([(self.PS(3 + d)[:, h * 128:(h + 1) * 128], ke[d][:, h * 128:(h + 1) * 128],
                               qe[d][:, h * 128:(h + 1) * 128], True, True) for h in range(NH)],
                             [("qe", d), ("ke", d)], [("ps", 3 + d)])
                    self.tt("dve", scT[d].rearrange("p (h c) -> p h c", h=NH),
                            self.PS(3 + d).rearrange("p (h c) -> p h c", h=NH), bcast_mid(self.SM[d], NH), ALU.mult,
                            [("ps", 3 + d), "masks"], [("scT", d)])
                pp = n % 2
                for j in range(2):
                    b = 3 + j
                    self.mms([(self.PS(b), H[:, kc, tsl], Wg[j][:, kc, :], kc == 0, kc == 7) for kc in range(8)],
                             [("w", 1 + j)] + hk, [("ps", b)])
                    js = slice(j * 512, (j + 1) * 512)
                    sgx = E[j]
                    self.act(sgx, self.PS(b), AF.Exp, [("ps", b)], [("E", j)], scale=-1.0)
                    self.act(sgx, sgx, AF.Ln, [("E", j)], [("E", j)], bias=1.0)
                    self.act(sgx, sgx, AF.Exp, [("E", j)], [("E", j)], scale=-1.0)
                    self.tt("dve", sg[pp][:, js], self.PS(b), sgx, ALU.mult, [("ps", b), ("E", j)], [("sg", pp, j)])

            def g2_mid(n):
                i = i0 + n
                for hp in range(2):
                    lst = []
                    for h in (2 * hp, 2 * hp + 1):
                        o = self.PS(5 + hp)[:, (h % 2) * 256:(h % 2 + 1) * 256]
                        hs = slice(h * 128, (h + 1) * 128)
                        vs = slice(h * 256, (h + 1) * 256)
                        lst.append((o, scT[0][:, hs], VT[:, i, vs], True, False))
                        lst.append((o, scT[1][:, hs], VT[:, i, vs], False, False))
                        lst.append((o, qe[0][:, hs], Sfb[:, vs], False, False))
                        lst.append((o, qe[1][:, hs], SBst[:, i, vs], False, True))
                    self.mms(lst, [("scT", 0), ("scT", 1), ("VT", i, hp), ("qe", 0), ("qe", 1), "Sfb", ("SBst", i)],
                             [("ps", 5 + hp)])
                self.copy("act", Sfb, Sst[0], [("S", 0)], ["Sfb"])

            def g2_tail(n):
                i = i0 + n
                hk = [(("H", i // 4), 0), (("H", i // 4), 1)]
                tsl = slice(i * 128, (i + 1) * 128)
                pp = n % 2
                for h in range(NH):
                    o = self.PS(5 + h // 2)[:, (h % 2) * 256:(h % 2 + 1) * 256]
                    self.act(junk, o, AF.Square, [("ps", 5 + h // 2)], ["gjunk", ("oss", h)], accum=stat[:, h:h + 1])
                self.act(stat[:, 4:8], stat[:, 0:4], AF.Ln, [("oss", h) for h in range(NH)], ["osq"], scale=1.0 / DV, bias=EPS)
                self.act(stat[:, 8:12], stat[:, 4:8], AF.Exp, ["osq"], ["ors"], scale=-0.5)
                for h in range(NH):
                    o = self.PS(5 + h // 2)[:, (h % 2) * 256:(h % 2 + 1) * 256]
                    self.stt(on[:, h * 256:(h + 1) * 256], o, stat[:, 8 + h:9 + h], self.gnb, ALU.mult, ALU.mult,
                             [("ps", 5 + h // 2), "ors", "params"], [("on", h)])
                self.tt("pool", ogt, on, sg[pp], ALU.mult, [("on", h) for h in range(NH)] + [("sg", pp, 0), ("sg", pp, 1)], ["ogt"])
                if job.name == "S":
                    self.convert_some(1)

            def g2_tail_p(n):
                i = i0 + n
                tsl = slice(i * 128, (i + 1) * 128)
                pT = self.PSbf(7, 1, 8)
                self.transposes([(pT[:, c, :], ogt[:, c * 128:(c + 1) * 128]) for c in range(8)], ["ogt"], [("ps", 7)])
                self.copy("act", ogf, pT, [("ps", 7)], ["ogf"])
                self.dma("sp", [(job.OGd.rearrange("c p t -> p c t")[:, :, tsl], ogf)], ["ogf"], [("OGd", job.name)], "ogf")
                mod_step(7)

            g2_front(0)
            g2_mid(0)
            for n in range(nch):
                g2_tail(n)
                if n + 1 < nch:
                    g2_front(n + 1)
                g2_tail_p(n)
                if n + 1 < nch:
                    g2_mid(n + 1)
            if job.state_out:
                self.dma("sp", [(T["nsg"][si, 0].rearrange("h k v -> k h v"), Sst[0].rearrange("p (h v) -> p h v", h=NH))],
                         [("S", 0)], ["nsg"], ("nsgo", 0))
        self.wslot = self.wslot[:self.NW]
        for k_ in (self.NW, self.NW + 1):
            self.wname.pop(k_, None)

    def phase_F(self, jobs):
        A, T, S = self.A, self.T, self.S
        self.modA = [A.f32(1024), A.f32(1024)]
        self.modB = [A.f32(1024) for _ in range(4)]
        xt = [A.f32(1024), A.f32(1024)]
        hb = [A.bf(1024), A.bf(1024)]
        tmp = [A.f32(1024), A.f32(1024)]
        junk = A.bf(1024)
        stat = A.f32(56)
        Hg = A.bf(8 * 512).rearrange("p (c t) -> p c t", c=8)
        YAw = A.bf(8 * 512).rearrange("p (c t) -> p c t", c=8)
        OGw = A.bf(8 * 512).rearrange("p (c t) -> p c t", c=8)
        MRG = A.bf(8 * 512).rearrange("p (c t) -> p c t", c=8)
        H2 = A.bf(8 * 512).rearrange("p (c t) -> p c t", c=8)
        HID = A.bf(32 * 512).rearrange("p (c t) -> p c t", c=32)
        x1 = [A.f32(1024) for _ in range(4)]
        sga = [A.f32(512), A.f32(512)]
        t12 = [A.f32(512), A.f32(512)]
        t1 = A.f32(4 * 512).rearrange("p (f t) -> p f t", f=4)
        rl = [A.bf(512), A.bf(512)]
        wslot_ctr = [0]
        HgK = [("Hg", 0), ("Hg", 1)]
        pairs = ((0, 1), (2, 3))

        def rstd(src, rkeys, s, o):
            c = 12 * s + o
            self.act(junk, src, AF.Square, rkeys, ["junk", ("ss", c)], accum=stat[:, c:c + 1])
            self.act(stat[:, c + 1:c + 2], stat[:, c:c + 1], AF.Ln, [("ss", c)], [("sq", c)], scale=1.0 / D, bias=EPS)
            self.act(stat[:, c + 2:c + 3], stat[:, c + 1:c + 2], AF.Exp, [("sq", c)], [("rs", c)], scale=-0.5)
            return stat[:, c + 2:c + 3], ("rs", c)

        def rstd4(src, rkeys, j, o):
            c = 32 + 6 * j + o
            self.act(junk, src, AF.Square, rkeys, ["junk", ("ss", c)], accum=stat[:, c:c + 1])
            self.act(stat[:, c + 1:c + 2], stat[:, c:c + 1], AF.Ln, [("ss", c)], [("sq", c)], scale=1.0 / D, bias=EPS)
            self.act(stat[:, c + 2:c + 3], stat[:, c + 1:c + 2], AF.Exp, [("sq", c)], [("rs", c)], scale=-0.5)
            return stat[:, c + 2:c + 3], ("rs", c)

        def xload(job, i, s):
            rows = self.tile_rows(job, job.x, i, job.perm)
            self.dma("sp", [(xt[s][lo:hi, :], src) for (lo, hi, src) in rows], (), [("xt", s)], ("xt", s))

        xs = xt
        pre_issued = [0]
        pT = self.PSbf(4, 4, 8)
        pbanks = [("ps", 4 + k) for k in range(4)]

        def s1_elem(job, gi, pr):
            sc1, b1 = self.modA
            for j in pr:
                rows = self.tile_rows(job, job.x, gi * 4 + j, job.perm)
                self.dma("sp", [(xs[j % 2][lo:hi, :], src) for (lo, hi, src) in rows], (), [("xt", j % 2)], ("xt", j % 2))
            rs = {}
            for j in pr:
                rs[j] = rstd(xs[j % 2], [("xt", j % 2)], j % 2, 0)
            for j in pr:
                s = j % 2
                self.stt(tmp[s], xs[s], rs[j][0], sc1, ALU.mult, ALU.mult, [("xt", s), rs[j][1], "modA"], [("tmp", s)])
            for j in pr:
                s = j % 2
                self.tt("pool", hb[s], tmp[s], b1, ALU.add, [("tmp", s), "modA"], [("hb", s)])

        def s1_T(pr):
            for j in pr:
                s = j % 2
                self.transposes([(pT[:, c, j * 128:(j + 1) * 128], hb[s][:, c * 128:(c + 1) * 128]) for c in range(8)],
                                [("hb", s)], pbanks)

        def s1_fin(job, gi):
            self.copy("act", Hg[:, 0:4, :], pT[:, 0:4, :], pbanks, [HgK[0]])
            self.copy("dve", Hg[:, 4:8, :], pT[:, 4:8, :], pbanks, [HgK[1]])
            gsl = slice(gi * 512, (gi + 1) * 512)
            self.dma("sp", [(YAw, job.YAd.rearrange("c p t -> p c t")[:, :, gsl]),
                            (OGw, job.OGd.rearrange("c p t -> p c t")[:, :, gsl])],
                     [("YAd", job.name), ("OGd", job.name)], ["YOw"], "YOw")

        groups = [(job, gi) for job in jobs for gi in range(job.NT // 512)]
        g1n, sc2, b2, g2n = self.modB
        for t_, (job, gi) in enumerate(groups):
            if True:
                NG = job.NT // 512
                if t_ == 0:
                    self.build_mod(job, (0, 1))
                    for pr in pairs:
                        s1_elem(job, 0, pr)
                        s1_T(pr)
                    s1_fin(job, 0)
                elif gi == 0:
                    self.build_mod(job, (1,))
                specs = self.f_specs()
                widx = [0]
                issued = [0]
                base = wslot_ctr[0]
                nxt = t_ + 1 < len(groups)
                njob, ngi = groups[t_ + 1] if nxt else (None, None)

                def prefetch(upto):
                    while issued[0] < min(upto, len(specs)):
                        k = issued[0]
                        sl_ = (base + k) % self.NW
                        self.wname[sl_] = specs[k][0] + (job.name, gi)
                        self.dma("sp", [(self.wslot[sl_], T["WS"][k])], [("WS", k)], [("w", sl_)], ("wf", sl_))
                        issued[0] += 1

                def nextw():
                    k = widx[0]
                    prefetch(k + self.NW - 1)
                    widx[0] += 1
                    sl = (base + k) % self.NW
                    return self.wslot[sl], ("w", sl)

                issued[0] = pre_issued[0]
                pre_issued[0] = 0
                prefetch(self.NW - 1)
                for blk in range(2):
                    for br, (rhsY, rkY) in enumerate(((YAw, ["YOw"]), (OGw, ["YOw"]))):
                        WY, kY = nextw()
                        WG, kG = nextw()
                        for half in range(2):
                            fls = (2 * half, 2 * half + 1)
                            for q, fl in enumerate(fls):
                                fs = slice(fl * 128, (fl + 1) * 128)
                                bk = 4 * half + q
                                self.mms([(self.PS(bk), WY[:, kc, fs], rhsY[:, kc, :], kc == 0, kc == 7) for kc in range(8)],
                                         [kY] + rkY, [("ps", bk)], name="brY")
                            for q, fl in enumerate(fls):
                                fs = slice(fl * 128, (fl + 1) * 128)
                                bk = 4 * half + 2 + q
                                self.mms([(self.PS(bk), WG[:, kc, fs], Hg[:, kc, :], kc == 0, kc == 7) for kc in range(8)],
                                         [kG] + HgK, [("ps", bk)], name="brG")
                            for q, fl in enumerate(fls):
                                self.act(sga[q], self.PS(4 * half + 2 + q), AF.Sigmoid, [("ps", 4 * half + 2 + q)], [("sga", q)])
                            for q, fl in enumerate(fls):
                                f = blk * 4 + fl
                                bk = 4 * half + q
                                if br == 0:
                                    self.tt("dve", t1[:, fl, :], self.PS(bk), sga[q], ALU.mult, [("ps", bk), ("sga", q)], [("t1", fl)])
                                else:
                                    self.tt("dve", t12[q], self.PS(bk), sga[q], ALU.mult, [("ps", bk), ("sga", q)], [("t12", q)])
                                    self.tt("pool", MRG[:, f, :], t1[:, fl, :], t12[q], ALU.add, [("t1", fl), ("t12", q)], ["MRG"])
                Wo0, kO0 = nextw()
                Wo1, kO1 = nextw()
                for pr in pairs:
                    for j in pr:
                        for cg, (Wo, kO) in enumerate(((Wo0, kO0), (Wo1, kO1))):
                            bk = 2 * j + cg
                            self.mms([(self.PS(bk), MRG[:, kc, j * 128:(j + 1) * 128], Wo[:, kc, :], kc == 0, kc == 7) for kc in range(8)],
                                     [kO, "MRG"], [("ps", bk)], name="mo")
                tmp4 = [tmp[0], tmp[1], t1[:, 0:2, :].rearrange("p f t -> p (f t)"), t1[:, 2:4, :].rearrange("p f t -> p (f t)")]
                tk4 = [[("tmp", 0)], [("tmp", 1)], [("t1", 0), ("t1", 1)], [("t1", 2), ("t1", 3)]]
                hb4 = [hb[0], hb[1], t12[0].bitcast(BF16), t12[1].bitcast(BF16)]
                hk4 = [[("hb", 0)], [("hb", 1)], [("t12", 0)], [("t12", 1)]]
                rs = {}
                rs2 = {}

                def m_stage(st_, j):
                    s_ = j % 2
                    if st_ == 0:
                        rs[j] = rstd4(self.PS(2 * j, 2), [("ps", 2 * j), ("ps", 2 * j + 1)], j, 0)
                    elif st_ == 1:
                        self.stt(tmp4[j], self.PS(2 * j, 2), rs[j][0], g1n, ALU.mult, ALU.mult,
                                 [("ps", 2 * j), ("ps", 2 * j + 1), rs[j][1], "modB"], tk4[j])
                    elif st_ == 2:
                        xload(job, gi * 4 + j, s_)
                        self.tt(("dve", "pool")[s_], x1[j], tmp4[j], xt[s_], ALU.add, tk4[j] + [("xt", s_)], [("x1", j)])
                    elif st_ == 3:
                        rs2[j] = rstd4(x1[j], [("x1", j)], j, 3)
                    elif st_ == 4:
                        self.stt(tmp4[j], x1[j], rs2[j][0], sc2, ALU.mult, ALU.mult, [("x1", j), rs2[j][1], "modB"], tk4[j])
                    elif st_ == 5:
                        self.tt(("dve", "pool")[s_], hb4[j], tmp4[j], b2, ALU.add, tk4[j] + ["modB"], hk4[j])
                    elif st_ == 6:
                        pT2 = self.PSbf(2 * j, 1, 8)
                        self.transposes([(pT2[:, c, :], hb4[j][:, c * 128:(c + 1) * 128]) for c in range(8)], hk4[j], [("ps", 2 * j)])
                    elif st_ == 7:
                        pT2 = self.PSbf(2 * j, 1, 8)
                        self.copy(("act", "dve")[s_], H2[:, :, j * 128:(j + 1) * 128], pT2, [("ps", 2 * j)], [("H2", j)])

                for step in range(8 + 3):
                    for j in range(4):
                        st_ = step - j
                        if 0 <= st_ < 8:
                            m_stage(st_, j)
                if nxt:
                    if njob is not job:
                        self.build_mod(njob, (0,))
                    s1_elem(njob, ngi, pairs[0])
                for wi in range(8):
                    W1, k1 = nextw()
                    for fl in range(4):
                        jf = wi * 4 + fl
                        bk = jf % 4
                        self.mms([(self.PS(bk), W1[:, kc, fl * 128:(fl + 1) * 128], H2[:, kc, :], kc == 0, kc == 7) for kc in range(8)],
                                 [k1] + [("H2", j_) for j_ in range(4)], [("ps", bk)], name="mlp1")
                        q = jf % 2
                        self.act(rl[q], self.PS(bk), AF.Relu, [("ps", bk)], [("rl", q)])
                        self.tt("pool", HID[:, jf, :], rl[q], rl[q], ALU.mult, [("rl", q)], [("HID", jf)])
                    if nxt and wi == 3:
                        s1_T(pairs[0])
                        s1_elem(njob, ngi, pairs[1])
                    if nxt and wi == 7:
                        s1_T(pairs[1])
                        s1_fin(njob, ngi)
                hidk = [("HID", jf) for jf in range(32)]
                for kg in range(4):
                    for cg in range(2):
                        W2, k2 = nextw()
                        for j in range(4):
                            bk = 2 * j + cg
                            self.mms([(self.PS(bk), HID[:, kg * 8 + kc, j * 128:(j + 1) * 128], W2[:, kc, :],
                                       kg == 0 and kc == 0, kg == 3 and kc == 7) for kc in range(8)],
                                     [k2] + hidk[kg * 8:(kg + 1) * 8], [("ps", bk)], name="mlp2")
                if nxt:
                    nbase = (base + len(specs)) % self.NW
                    for k in range(self.NW - 1):
                        sl_ = (nbase + k) % self.NW
                        self.wname[sl_] = specs[k][0] + ("pre", job.name, gi)
                        self.dma("sp", [(self.wslot[sl_], T["WS"][k])], [("WS", k)], [("w", sl_)], ("wf", sl_))
                    pre_issued[0] = self.NW - 1
                for pr in pairs:
                    rs = {}
                    for j in pr:
                        rs[j] = rstd(self.PS(2 * j, 2), [("ps", 2 * j), ("ps", 2 * j + 1)], j % 2, 9)
                    for j in pr:
                        s = j % 2
                        self.stt(tmp[s], self.PS(2 * j, 2), rs[j][0], g2n, ALU.mult, ALU.mult,
                                 [("ps", 2 * j), ("ps", 2 * j + 1), rs[j][1], "modB"], [("tmp", s)])
                    for j in pr:
                        s = j % 2
                        self.tt(("dve", "pool")[s], xt[s], tmp[s], x1[j], ALU.add, [("tmp", s), ("x1", j)], [("xt", s)])
                    for j in pr:
                        s = j % 2
                        rows = self.tile_rows(job, job.y, gi * 4 + j, job.perm)
                        self.dma("sp", [(dst, xt[s][lo:hi, :]) for (lo, hi, dst) in rows], [("xt", s)], ["yout"], ("xo", s))
                wslot_ctr[0] = (base + len(specs)) % self.NW

    def f_specs(self):
        T = self.T
        specs = []
        for blk in range(2):
            cs = slice(blk * 512, (blk + 1) * 512)
            specs += [(("lru_up", blk), T["lru_up"][:, cs]),
                      (("w_in", "ga", blk), T["w_in"][:, OFF["ga"] + blk * 512: OFF["ga"] + (blk + 1) * 512]),
                      (("gla_up", blk), T["gla_up"][:, cs]),
                      (("w_in", "gb", blk), T["w_in"][:, OFF["gb"] + blk * 512: OFF["gb"] + (blk + 1) * 512])]
        specs += [(("w_out", cg), T["w_out"][:, cg * 512:(cg + 1) * 512]) for cg in range(2)]
        specs += [(("w1", wi), T["mlp_w1"][:, wi * 512:(wi + 1) * 512]) for wi in range(8)]
        specs += [(("w2", kg, cg), T["mlp_w2"][kg * 1024:(kg + 1) * 1024, cg * 512:(cg + 1) * 512])
                  for kg in range(4) for cg in range(2)]
        return specs

    def convert_some(self, n):
        for _ in range(n):
            if not self.conv_pending:
                return
            k, src = self.conv_pending.pop(0)
            self.dma("pool", [(self.T["WS"][k], src.rearrange("(c p) n -> p c n", p=128))], (), [("WS", k)], "WSconv")

    def rms_stat(self, src, rkeys, junk, stat, o, n):
        self.act(junk[:, 0:n], src, AF.Square, rkeys, ["junk", ("ss", o)], accum=stat[:, o:o + 1])
        self.act(stat[:, o + 1:o + 2], stat[:, o:o + 1], AF.Ln, [("ss", o)], [("sq", o)], scale=1.0 / n, bias=EPS)
        self.act(stat[:, o + 2:o + 3], stat[:, o + 1:o + 2], AF.Exp, [("sq", o)], [("rs", o)], scale=-0.5)


_CACHE = {}


def build_program():
    nc = bass.Bass("TRN2", target_bir_lowering=False)
    T = {}

    def inp(name, shape):
        T[name] = nc.dram_tensor(name, list(shape), F32, kind="ExternalInput").ap()

    def outp(name, shape):
        T[name] = nc.dram_tensor(name, list(shape), F32, kind="ExternalOutput").ap()

    inp("xp", (512, D))
    inp("xs", (2048, D))
    inp("st_lru", (2, D))
    inp("st_gla", (2, NH, DK, DV))
    inp("cvec", (2, D))
    inp("w_mod", (D, 6 * D))
    inp("b_mod", (6 * D,))
    inp("norm_g", (4, D))
    inp("w_in", (D, 7200))
    inp("conv_w", (4, D))
    inp("conv_b", (D,))
    inp("lru_wa", (2, 16, 64, 64))
    inp("lru_ba", (2, D))
    inp("lru_wx", (2, 16, 64, 64))
    inp("lru_bx", (2, D))
    inp("lru_L", (2, D))
    inp("lru_up", (D, D))
    inp("gla_w2", (2, 16, 512))
    inp("gla_b2", (2, 512))
    inp("gla_norm_g", (DV,))
    inp("gla_up", (D, D))
    inp("w_out", (D, D))
    inp("mlp_w1", (D, DFF))
    inp("mlp_w2", (DFF, D))
    outp("yp", (512, D))
    outp("ys", (2048, D))
    outp("nsl", (2, 2, D))
    outp("nsg", (2, 2, NH, DK, DV))
    kind = dict(kind="ExternalOutput") if DEBUG else {}
    T["modd"] = nc.dram_tensor("modd", [2, 6 * D], F32, **kind).ap()
    T["WS"] = nc.dram_tensor("WS", [26, 128, 8, 512], BF16).ap()
    T["YAd_P"] = nc.dram_tensor("YAd_P", [8, 128, 512], BF16, **kind).ap()
    T["OGd_P"] = nc.dram_tensor("OGd_P", [8, 128, 512], BF16, **kind).ap()
    T["YAd_S"] = nc.dram_tensor("YAd_S", [8, 128, 2048], BF16, **kind).ap()
    T["OGd_S"] = nc.dram_tensor("OGd_S", [8, 128, 2048], BF16, **kind).ap()
    B = Builder(nc)
    with nc.allow_non_contiguous_dma(reason="small strided parameter / state layouts"):
        stats = B.build(T)
    return nc, stats, B


def make_in_maps(inputs):
    f = lambda a: np.ascontiguousarray(np.asarray(a, dtype=np.float32))
    x_prompt = f(inputs["x_prompt"])
    x_sample = f(inputs["x_sample"])
    state_lru = f(inputs["state_lru"])
    state_gla = f(inputs["state_gla"])
    c = f(inputs["c"])
    c_ctx = f(inputs["c_ctx"])
    shared = {
        "w_mod": f(inputs["w_mod"])[0], "b_mod": f(inputs["b_mod"])[0], "norm_g": f(inputs["norm_g"])[0],
        "w_in": f(inputs["w_in"])[0], "conv_w": f(inputs["conv_w"])[0], "conv_b": f(inputs["conv_b"])[0],
        "lru_wa": f(inputs["lru_wa"])[0], "lru_ba": f(inputs["lru_ba"])[0], "lru_wx": f(inputs["lru_wx"])[0],
        "lru_bx": f(inputs["lru_bx"])[0], "lru_L": f(inputs["lru_L"])[0], "lru_up": f(inputs["lru_up"])[0],
        "gla_w2": f(inputs["gla_w2"])[0], "gla_b2": f(inputs["gla_b2"])[0], "gla_norm_g": f(inputs["gla_norm_g"])[0],
        "gla_up": f(inputs["gla_up"])[0], "w_out": f(inputs["w_out"])[0], "mlp_w1": f(inputs["mlp_w1"])[0],
        "mlp_w2": f(inputs["mlp_w2"])[0],
    }
    in_maps = []
    for k in range(NCORES):
        m = dict(shared)
        m["xp"] = np.ascontiguousarray(x_prompt[2 * k:2 * k + 2].reshape(512, D))
        m["xs"] = np.ascontiguousarray(x_sample[k])
        m["st_lru"] = np.ascontiguousarray(state_lru[k, 0])
        m["st_gla"] = np.ascontiguousarray(state_gla[k, 0])
        m["cvec"] = np.ascontiguousarray(np.stack([c_ctx, c[k]], axis=0))
        in_maps.append(m)
    return in_maps


def kernel(**inputs):
    if "nc" not in _CACHE:
        _CACHE["nc"] = build_program()[0]
    nc = _CACHE["nc"]
    in_maps = make_in_maps(inputs)
    res = run_bass_kernel_spmd(nc, in_maps, core_ids=list(range(NCORES)))
    rs = res.results
    y_prompt = np.concatenate([np.asarray(r["yp"]).reshape(2, 256, D) for r in rs], axis=0).astype(np.float32)
    y_sample = np.stack([np.asarray(r["ys"]) for r in rs], axis=0).astype(np.float32)
    nsl = np.concatenate([np.asarray(r["nsl"]).reshape(2, 1, 2, D) for r in rs], axis=0).astype(np.float32)
    nsg = np.concatenate([np.asarray(r["nsg"]).reshape(2, 1, 2, NH, DK, DV) for r in rs], axis=0).astype(np.float32)
    return (y_prompt, y_sample, nsl, nsg)
```

```python
import math
from contextlib import ExitStack

import numpy as np
import concourse.bass as bass
import concourse.mybir as mybir
from concourse.bass_utils import run_bass_kernel_spmd

F32 = mybir.dt.float32
BF16 = mybir.dt.bfloat16
AF = mybir.ActivationFunctionType
ALU = mybir.AluOpType

COMPUTE = ("pe", "act", "dve", "pool")

D = 1024
DFF = 4096
NH = 4
DK = 128
DV = 256
EPS = 1e-6
OFF = dict(zx=0, zg=1024, q=2048, k=2560, v=3072, g=4096, lr=5120, ga=5152, gb=6176)
ARENA_W = 51900
NCORES = 8
DEBUG = False


class Op:
    __slots__ = ("eng", "fn", "reads", "writes", "dma", "semkey", "deps", "sig", "sigval", "name", "ninstr")

    def __init__(self, eng, fn, reads, writes, dma, semkey, name):
        self.eng = eng
        self.fn = fn
        self.reads = reads
        self.writes = writes
        self.dma = dma
        self.semkey = semkey
        self.deps = []
        self.sig = False
        self.sigval = None
        self.name = name


class Sched:
    def __init__(self, nc):
        self.nc = nc
        self.ops = {e: [] for e in ("pe", "act", "dve", "pool", "sp")}
        self.last_w = {}
        self.readers = {}
        self.bar_deps = {}
        self.all_ops = []
        self.last_dma = {}

    def add(self, eng, fn, reads=(), writes=(), dma=0, semkey=None, name=""):
        op = Op(eng, fn, tuple(reads), tuple(writes), dma, semkey, name)
        if dma:
            assert semkey is not None
        deps = {}
        for r in op.reads:
            y = self.last_w.get(r)
            if y is not None:
                deps[id(y)] = (y, True)
        for w in op.writes:
            y = self.last_w.get(w)
            if y is not None and id(y) not in deps:
                deps[id(y)] = (y, False)
            for y in self.readers.get(w, ()):
                if id(y) not in deps:
                    deps[id(y)] = (y, False)
        if eng in self.bar_deps:
            for y in self.bar_deps.pop(eng):
                deps[id(y)] = (y, True)
        for r in op.reads:
            self.readers.setdefault(r, []).append(op)
        for w in op.writes:
            self.last_w[w] = op
            self.readers[w] = []
        for y, raw in deps.values():
            if y is op:
                continue
            if (not y.dma) and (not op.dma) and y.eng == eng and eng == "pe" and not raw:
                continue
            op.deps.append(y)
            y.sig = True
        self.ops[eng].append(op)
        self.all_ops.append(op)
        if dma:
            self.last_dma[semkey] = op
        return op

    def barrier(self):
        deps = []
        for e in self.ops:
            for op in reversed(self.ops[e]):
                if not op.dma:
                    deps.append(op)
                    break
        for k, op in self.last_dma.items():
            deps.append(op)
        for e in self.ops:
            self.bar_deps[e] = list(deps)

    def emit(self, final_eng="sp"):
        nc = self.nc
        self.barrier()
        self.add(final_eng, None, name="final")
        for e, lst in self.ops.items():
            cnt = 0
            for op in lst:
                if (not op.dma) and op.sig:
                    cnt += 1
                    op.sigval = cnt
        dma_keys = {}
        for op in self.all_ops:
            if op.dma:
                dma_keys[op.semkey] = dma_keys.get(op.semkey, 0) + 16 * op.dma
                op.sigval = dma_keys[op.semkey]
        sems = {}
        es = ExitStack()
        for e in COMPUTE:
            sems[("eng", e)] = es.enter_context(nc.semaphore("s_" + e))
        for i, k in enumerate(dma_keys):
            sems[("dma", k)] = es.enter_context(nc.semaphore("d%d" % i))
        self.nsems = len(sems)
        stats = {}
        with nc.Block() as block:
            def body(ename):
                def run(eng):
                    waited = {}
                    nw = 0
                    for op in self.ops[ename]:
                        for y in op.deps:
                            key = ("dma", y.semkey) if y.dma else ("eng", y.eng)
                            if waited.get(key, 0) >= y.sigval:
                                continue
                            waited[key] = y.sigval
                            eng.wait_ge(sems[key], y.sigval)
                            nw += 1
                        if op.fn is None:
                            continue
                        if op.dma:
                            op.fn(eng, sems[("dma", op.semkey)])
                        else:
                            ins = op.fn(eng)
                            if op.sig:
                                ins.then_inc(sems[("eng", ename)], 1)
                    stats[ename] = (len(self.ops[ename]), nw)
                return run
            block.tensor(body("pe"))
            block.scalar(body("act"))
            block.vector(body("dve"))
            block.gpsimd(body("pool"))
            block.sync(body("sp"))
        es.close()
        return stats


class Arena:
    def __init__(self, ap, n):
        self.ap = ap
        self.n = n
        self.off = 0
        self.peak = 0

    def f32(self, words, parts=128):
        off = self.off
        self.off += words
        self.peak = max(self.peak, self.off)
        assert self.off <= self.n, ("arena overflow", self.off, self.n)
        return self.ap[0:parts, off:off + words]

    def bf(self, elems, parts=128):
        return self.f32((elems + 1) // 2, parts).bitcast(BF16)

    def mark(self):
        return self.off

    def release(self, m):
        self.off = m


def rev(ap):
    a = ap.ap
    assert len(a) == 2
    return bass.AP(ap.tensor, ap.offset + (a[1][1] - 1) * a[1][0], [list(a[0]), [-a[1][0], a[1][1]]])


def bcast_mid(ap, n):
    a = ap.ap
    assert len(a) == 2
    return bass.AP(ap.tensor, ap.offset, [list(a[0]), [0, n], list(a[1])])


class Job:
    pass


class Builder:
    def __init__(self, nc):
        self.nc = nc
        self.S = Sched(nc)
        self.es = ExitStack()
        arena_t = self.es.enter_context(nc.sbuf_tensor("arena", [128, ARENA_W], F32))
        self.A = Arena(arena_t, ARENA_W)
        self.psum = self.es.enter_context(nc.psum_tensor("psum", [128, 8 * 512], F32))
        self.wname = {}
        self.dbg = {}

    def PS(self, b, n=1):
        return self.psum[:, b * 512:(b + n) * 512]

    def PSbf(self, b, nb, c):
        return self.psum[:, b * 512:(b + nb) * 512].bitcast(BF16).rearrange("p (c t) -> p c t", c=c)

    def act(self, out, in_, func, r, w, scale=None, bias=None, accum=None, name=""):
        kw = {}
        if scale is not None:
            kw["scale"] = scale
        if bias is not None:
            kw["bias"] = bias
        if accum is not None:
            kw["accum_out"] = accum
        self.S.add("act", lambda e: e.activation(out=out, in_=in_, func=func, **kw), r, w, name=name)

    def tt(self, eng, out, a, b, op, r, w, name=""):
        self.S.add(eng, lambda e: e.tensor_tensor(out=out, in0=a, in1=b, op=op), r, w, name=name)

    def stt(self, out, in0, scalar, in1, op0, op1, r, w, name=""):
        self.S.add("dve", lambda e: e.scalar_tensor_tensor(out=out, in0=in0, scalar=scalar, in1=in1, op0=op0, op1=op1),
                   r, w, name=name)

    def ts(self, eng, out, in0, s1, s2, op0, op1, r, w, name=""):
        if op1 is None:
            self.S.add(eng, lambda e: e.tensor_scalar(out=out, in0=in0, scalar1=s1, scalar2=None, op0=op0), r, w, name=name)
        else:
            self.S.add(eng, lambda e: e.tensor_scalar(out=out, in0=in0, scalar1=s1, scalar2=s2, op0=op0, op1=op1), r, w, name=name)

    def copy(self, eng, out, in_, r, w, name=""):
        if eng == "act":
            self.act(out, in_, AF.Copy, r, w, name=name)
        else:
            self.S.add(eng, lambda e: e.tensor_copy(out=out, in_=in_), r, w, name=name)

    def memset(self, eng, ap, val, w):
        self.S.add(eng, lambda e: e.memset(ap, val), (), w)

    def recip(self, out, in_, r, w):
        self.S.add("dve", lambda e: e.reciprocal(out=out, in_=in_), r, w)

    def mms(self, lst, r, w, name=""):
        def fn(e):
            ins = None
            for (o, l, rr, st, sp) in lst:
                ins = e.matmul(o, lhsT=l, rhs=rr, start=st, stop=sp)
            return ins
        op = self.S.add("pe", fn, r, w, name=name)
        op.ninstr = len(lst)

    def transposes(self, lst, r, w):
        ident = self.ident

        def fn(e):
            ins = None
            for (o, i) in lst:
                ins = e.transpose(out=o, in_=i, identity=ident)
            return ins
        op = self.S.add("pe", fn, tuple(r) + ("ident",), w, name="T")
        op.ninstr = len(lst)

    def dma(self, q, pairs, r, w, key):
        def fn(e, sem):
            for (o, i) in pairs:
                e.dma_start(out=o, in_=i).then_inc(sem, 16)
        self.S.add(q, fn, r, w, dma=len(pairs), semkey=key)

    def load_w(self, slot, name, src, ncols=512):
        if self.wname.get(slot) == name:
            return self.wslot[slot]
        self.wname[slot] = name
        dst = self.wslot[slot][:, :, 0:ncols]
        self.dma("pool", [(dst, src.rearrange("(c p) n -> p c n", p=128))], (), [("w", slot)], ("w", slot))
        return self.wslot[slot]

    def build(self, T):
        nc, S, A = self.nc, self.S, self.A
        self.T = T
        self.ident = A.bf(128)
        onesf = A.f32(128)
        self.memset("pool", onesf, 1.0, ["onesf"])
        S.add("pool", lambda e: e.affine_select(out=self.ident, in_=onesf, pattern=[[1, 128]], compare_op=ALU.is_equal,
                                                fill=0.0, base=0, channel_multiplier=-1), ["onesf"], ["ident"])
        self.identf = A.f32(128)
        S.add("pool", lambda e: e.affine_select(out=self.identf, in_=onesf, pattern=[[1, 128]], compare_op=ALU.is_equal,
                                                fill=0.0, base=0, channel_multiplier=-1), ["onesf"], ["identf"])
        cval = A.f32(128)
        self.memset("pool", cval, -1.0 / 16.0, ["cval"])
        self.negcol = cval[:, 0:1]
        self.Mc = [A.f32(128), A.f32(128)]
        self.Mt = [A.f32(128), A.f32(128)]
        self.SM = [A.f32(128), A.f32(128)]
        specs = [(self.Mc[0], cval, 1, -1, ALU.is_ge), (self.Mc[1], cval, -1, 1, ALU.is_ge),
                 (self.Mt[0], cval, -1, 1, ALU.is_gt), (self.Mt[1], cval, 1, -1, ALU.is_gt),
                 (self.SM[0], onesf, 1, -1, ALU.is_ge), (self.SM[1], onesf, -1, 1, ALU.is_ge)]
        for (o, i, st, cm, cmp) in specs:
            S.add("pool", lambda e, o=o, i=i, st=st, cm=cm, cmp=cmp: e.affine_select(
                out=o, in_=i, pattern=[[st, 128]], compare_op=cmp, fill=0.0, base=0, channel_multiplier=cm),
                ["onesf", "cval"], ["masks"])
        self.cw = A.f32(32).rearrange("p (c k) -> p c k", k=4)
        self.cb = A.f32(8)
        self.lba = A.f32(16).rearrange("p (d c) -> p d c", d=2)
        self.lbx = A.f32(16).rearrange("p (d c) -> p d c", d=2)
        self.lL = A.f32(16).rearrange("p (d c) -> p d c", d=2)
        self.cL = A.f32(16).rearrange("p (d c) -> p d c", d=2)
        self.h0 = A.f32(16).rearrange("p (d c) -> p d c", d=2)
        self.gnb = A.f32(256)
        pl = [(self.cw[:, :, k], T["conv_w"][k].rearrange("(c p) -> p c", p=128)) for k in range(4)]
        pl.append((self.cb, T["conv_b"].rearrange("(c p) -> p c", p=128)))
        for (dst, nm) in ((self.lba, "lru_ba"), (self.lbx, "lru_bx"), (self.lL, "lru_L"), (self.h0, "st_lru")):
            for d in range(2):
                pl.append((dst[:, d, :], T[nm][d].rearrange("(c p) -> p c", p=128)))
        pl.append((self.gnb, T["gla_norm_g"].partition_broadcast(128)))
        self.dma("sp", pl, (), ["params"], "params")
        tmp16 = A.f32(16).rearrange("p (d c) -> p d c", d=2)
        self.act(tmp16, self.lL, AF.Exp, ["params"], ["tmp16"], scale=-1.0)
        self.act(tmp16, tmp16, AF.Ln, ["tmp16"], ["tmp16b"], bias=1.0)
        self.ts("dve", self.cL, tmp16, -8.0, None, ALU.mult, None, ["tmp16b"], ["cL"])
        self.hcL = A.f32(16).rearrange("p (d c) -> p d c", d=2)
        self.hba = A.f32(16).rearrange("p (d c) -> p d c", d=2)
        self.hbx = A.f32(16).rearrange("p (d c) -> p d c", d=2)
        self.ts("dve", self.hcL, self.cL, 0.5, None, ALU.mult, None, ["cL"], ["hpar"])
        self.ts("dve", self.hba, self.lba, 0.5, None, ALU.mult, None, ["params", "hpar"], ["hpar"])
        self.ts("dve", self.hbx, self.lbx, 0.5, None, ALU.mult, None, ["params", "hpar"], ["hpar"])
        self.w2aug = []
        for d in range(2):
            t = A.bf(512)
            self.w2aug.append(t)
            self.dma("pool", [(t[0:16, :], T["gla_w2"][d]), (t[16:17, :], T["gla_b2"][d:d + 1, :])], (), [("w2aug", d)], ("w2aug", d))
        self.NW = 4
        wraw = A.f32(self.NW * 2048)
        self.wslot = [wraw[:, i * 2048:(i + 1) * 2048].bitcast(BF16).rearrange("p (c n) -> p c n", c=8) for i in range(self.NW)]
        self.wpair = [wraw[:, i * 4096:(i + 1) * 4096].rearrange("p (c n) -> p c n", c=8) for i in range(self.NW // 2)]
        self.wlr = A.bf(8 * 32).rearrange("p (c n) -> p c n", c=8)
        self.dma("pool", [(self.wlr, T["w_in"][:, OFF["lr"]:OFF["lr"] + 32].rearrange("(c p) n -> p c n", p=128))],
                 (), ["wlr"], "wlr")
        self.BD = {}
        for wi, wname in enumerate(("lru_wa", "lru_wx")):
            for d in range(2):
                t = A.bf(8 * 128).rearrange("p (c j) -> p c j", c=8)
                self.BD[(wi, d)] = t
                key = ("BD", wi, d)
                self.memset("pool", t, 0.0, [key])
                v = T[wname][d].rearrange("(c two) i j -> two i c j", two=2)
                self.dma("pool", [(t[0:64, :, 0:64], v[0]), (t[64:128, :, 64:128], v[1])], (), [key], key)
        self.setup_mod()
        base_mark = A.mark()
        mt = self.mod_alloc()
        for cg in range(4):
            self.mod_tile(mt, cg, cg)
        self.mod_pending = list(range(4, 12))
        S.barrier()
        A.release(base_mark)
        jP = Job()
        jP.name, jP.NT, jP.seqs, jP.convL, jP.perm = "P", 512, [(0, 256), (256, 256)], 256, False
        jP.x, jP.y, jP.mrow, jP.state_in, jP.state_out = T["xp"], T["yp"], 0, False, True
        jP.YAd, jP.OGd = T["YAd_P"], T["OGd_P"]
        jS = Job()
        jS.name, jS.NT, jS.seqs, jS.convL, jS.perm = "S", 2048, [(0, 2048)], 64, True
        jS.x, jS.y, jS.mrow, jS.state_in, jS.state_out = T["xs"], T["ys"], 1, True, False
        jS.YAd, jS.OGd = T["YAd_S"], T["OGd_S"]
        self.conv_pending = [(k, sp[1]) for k, sp in enumerate(self.f_specs())]
        self.Hbuf = A.bf(8 * 2048).rearrange("p (c t) -> p c t", c=8)
        mark1 = A.mark()
        for job in (jP, jS):
            self.phase_S1(job, self.Hbuf, perm=job.perm, with_mod=(job is jS))
            S.barrier()
            A.release(mark1)

            self.phase_G(job)
            S.barrier()
            A.release(mark1)
            if job.perm:
                self.phase_S1(job, self.Hbuf, perm=False)
                S.barrier()
                A.release(mark1)
            self.phase_L(job)
            S.barrier()
            A.release(mark1)
        self.convert_some(100)
        assert not self.mod_pending
        S.barrier()
        A.release(base_mark)
        self.phase_F([jS, jP])
        stats = S.emit()
        self.es.close()
        return stats

    def setup_mod(self):
        A, T = self.A, self.T
        cvT = A.f32(16).rearrange("p (c r) -> p c r", r=2)
        self.scT = A.bf(16).rearrange("p (c r) -> p c r", r=2)
        self.dma("sp", [(cvT[:, :, r], T["cvec"][r].rearrange("(c p) -> p c", p=128)) for r in range(2)], (), ["cvT"], "cvT")
        self.act(self.scT, cvT, AF.Silu, ["cvT"], ["scT"])

    def mod_alloc(self):
        A = self.A
        return [dict(bm=A.f32(512, parts=2), ng=A.f32(512, parts=2), rows=A.f32(512, parts=2)) for _ in range(2)]

    def mod_tile(self, mt, cg, bank, xslot=None):
        T = self.T
        t = mt[cg % 2]
        k = cg % 2
        seg, hs = cg // 2, (cg % 2) * 512
        cs = slice(cg * 512, (cg + 1) * 512)
        ngi = {1: 0, 2: 1, 4: 2, 5: 3}.get(seg)
        pairs_ = [(t["bm"], T["b_mod"][cs].partition_broadcast(2))]
        if ngi is not None:
            pairs_.append((t["ng"], T["norm_g"][ngi][hs:hs + 512].partition_broadcast(2)))
        self.dma("sp", pairs_, (), [("mbm", k)], ("mbm", k))
        if xslot is None:
            sl = cg % self.NW
            wt = self.load_w(sl, ("wmod", cg), T["w_mod"][:, cs])
            wk = [("w", sl)]
        else:
            wt = xslot
            wk = ["wx"]
            self.dma("pool", [(wt, T["w_mod"][:, cs].rearrange("(c p) n -> p c n", p=128))], (), wk, "wx")
        ps = self.PS(bank)[0:2, :]
        self.mms([(ps, self.scT[:, kc, :], wt[:, kc, :], kc == 0, kc == 7) for kc in range(8)],
                 ["scT"] + wk, [("ps", bank)])
        self.tt("dve", t["rows"], ps, t["bm"], ALU.add, [("ps", bank), ("mbm", k)], [("mrows", k)])
        if seg in (1, 4):
            self.stt(t["rows"], t["rows"], 1.0, t["ng"], ALU.add, ALU.mult, [("mrows", k), ("mbm", k)], [("mrows", k)])
        elif seg in (2, 5):
            self.tt("dve", t["rows"], t["rows"], t["ng"], ALU.mult, [("mrows", k), ("mbm", k)], [("mrows", k)])
        self.dma("sp", [(T["modd"][:, cs], t["rows"])], [("mrows", k)], [("modd", cg)], ("mrows", k))

    def build_mod(self, job, which, ngt=None):
        T = self.T
        r = job.mrow

        def seg(i):
            return T["modd"][r, i * 1024:(i + 1) * 1024].partition_broadcast(128)

        def mk(*segs):
            return [("modd", 2 * i + h) for i in segs for h in range(2)]
        if 0 in which:
            sc1, b1 = self.modA
            self.dma("sp", [(sc1, seg(1)), (b1, seg(0))], mk(0, 1), ["modA"], "modA")
        if 1 in which:
            g1n, sc2, b2, g2n = self.modB
            self.dma("sp", [(g1n, seg(2)), (sc2, seg(4)), (b2, seg(3)), (g2n, seg(5))], mk(2, 3, 4, 5), ["modB"], "modB")

    def tile_rows(self, job, dram, i, perm):
        if not perm:
            return [(0, 128, dram[i * 128:(i + 1) * 128, :])]
        v = dram.rearrange("(r w) d -> w r d", w=64)
        return [(32 * k, 32 * k + 32, v[4 * i + k]) for k in range(4)]

    def s1_alloc(self):
        A = self.A
        st = {}
        st["xt"] = [A.f32(1024) for _ in range(8)]
        st["t"] = [A.f32(1024) for _ in range(4)]
        st["hb"] = [A.bf(1024) for _ in range(8)]
        st["junk"] = A.bf(1024)
        st["stat"] = A.f32(24)
        return st

    def s1_pre(self, st, job, gi, perm):
        sc1, b1 = self.modA
        stat = st["stat"]
        xo = (gi % 2) * 4
        xts = st["xt"][xo:xo + 4]
        hbs = st["hb"][xo:xo + 4]
        so = (gi % 2) * 12
        for j in range(4):
            rows = self.tile_rows(job, job.x, gi * 4 + j, perm)
            self.dma("sp", [(xts[j][lo:hi, :], src) for (lo, hi, src) in rows], (), [("xt", xo + j)], ("xt", xo + j))
        for j in range(4):
            self.act(st["junk"], xts[j], AF.Square, [("xt", xo + j)], ["junk", ("ss", xo + j)], accum=stat[:, so + j:so + j + 1])
        for j in range(4):
            self.act(stat[:, so + 4 + j:so + 5 + j], stat[:, so + j:so + j + 1], AF.Ln, [("ss", xo + j)], [("sq", xo + j)],
                     scale=1.0 / D, bias=EPS)
            self.act(stat[:, so + 8 + j:so + 9 + j], stat[:, so + 4 + j:so + 5 + j], AF.Exp, [("sq", xo + j)], [("rs", xo + j)], scale=-0.5)
        for j in range(4):
            self.stt(st["t"][j], xts[j], stat[:, so + 8 + j:so + 9 + j], sc1, ALU.mult, ALU.mult,
                     [("xt", xo + j), ("rs", xo + j), "modA"], [("s1t", j)])
        for j in range(4):
            self.tt(("pool", "dve")[j % 2], hbs[j], st["t"][j], b1, ALU.add, [("s1t", j), "modA"], [("hb", xo + j)])

    def s1_post(self, st, gi, Hdst, hkey):
        pbank = (gi % 2) * 4
        xo = (gi % 2) * 4
        hbs = st["hb"][xo:xo + 4]
        pT = self.PSbf(pbank, 4, 8)
        banks = [("ps", pbank + k) for k in range(4)]
        for j in range(4):
            self.transposes([(pT[:, c, j * 128:(j + 1) * 128], hbs[j][:, c * 128:(c + 1) * 128]) for c in range(8)],
                            [("hb", xo + j)], banks)
        self.copy("act", Hdst[:, 0:4, :], pT[:, 0:4, :], banks, [(hkey, 0)])
        self.copy("dve", Hdst[:, 4:8, :], pT[:, 4:8, :], banks, [(hkey, 1)])

    def phase_S1(self, job, H, perm, with_mod=False):
        self.modA = [self.A.f32(1024), self.A.f32(1024)]
        self.build_mod(job, (0,))
        st = self.s1_alloc()
        NG = job.NT // 512
        self.s1_pre(st, job, 0, perm)
        for gi in range(NG):
            if gi + 1 < NG:
                self.s1_pre(st, job, gi + 1, perm)
            self.s1_post(st, gi, H[:, :, gi * 512:(gi + 1) * 512], ("H", gi))

    def phase_L(self, job):
        A, T, S = self.A, self.T, self.S
        NT = job.NT
        TT = NT // 512
        H = self.Hbuf
        zx = [A.f32(NT), A.f32(NT)]
        xc = [A.f32(NT), A.f32(NT)]
        xcb = [A.bf(NT), A.bf(NT)]
        gz = [A.bf(NT), A.bf(NT)]
        a_ = [A.f32(NT), A.f32(NT)]
        iu = [A.f32(NT), A.f32(NT)]
        s_ = [A.f32(NT), A.f32(NT)]
        ya = [A.bf(NT), A.bf(NT)]
        nseq = len(job.seqs)
        lruo = A.f32(nseq * 16).rearrange("p (s d c) -> p s d c", s=nseq, d=2)
        L = job.convL
        LN_HALF = math.log(0.5)

        def seg(ap, lo, hi):
            return ap.rearrange("p (s l) -> p s l", l=L)[:, :, lo:hi]
        Hk = [[(("H", tt), 0), (("H", tt), 1)] for tt in range(TT)]
        wts = {}
        for blk in range(2):
            wts[("x", blk)] = self.load_w(blk * 2, ("w_in", "zx", blk), T["w_in"][:, OFF["zx"] + blk * 512: OFF["zx"] + (blk + 1) * 512])
            wts[("g", blk)] = self.load_w(blk * 2 + 1, ("w_in", "zg", blk), T["w_in"][:, OFF["zg"] + blk * 512: OFF["zg"] + (blk + 1) * 512])

        def mm_zx(c, tt):
            blk, cbk = c // 4, c % 4
            wx = wts[("x", blk)]
            ts_ = slice(tt * 512, (tt + 1) * 512)
            b = tt % 2
            self.mms([(self.PS(b), wx[:, kc, cbk * 128:(cbk + 1) * 128], H[:, kc, ts_], kc == 0, kc == 7) for kc in range(8)],
                     [("w", blk * 2)] + Hk[tt], [("ps", b)])

        def mm_zg(c, tt):
            blk, cbk = c // 4, c % 4
            wg = wts[("g", blk)]
            ts_ = slice(tt * 512, (tt + 1) * 512)
            b = 2 + tt % 2
            self.mms([(self.PS(b), wg[:, kc, cbk * 128:(cbk + 1) * 128], H[:, kc, ts_], kc == 0, kc == 7) for kc in range(8)],
                     [("w", blk * 2 + 1)] + Hk[tt], [("ps", b)])

        NHEAD = min(2, TT)

        def stage_A_head(c):
            for tt in range(NHEAD):
                mm_zx(c, tt)
            for tt in range(NHEAD):
                mm_zg(c, tt)

        def stage_A1(c):
            p = c % 2
            for tt in range(TT):
                ts_ = slice(tt * 512, (tt + 1) * 512)
                if tt >= NHEAD:
                    mm_zx(c, tt)
                self.copy("dve", zx[p][:, ts_], self.PS(tt % 2), [("ps", tt % 2)], [("zx", p, tt)])
            zxk = [("zx", p, tt) for tt in range(TT)]
            self.ts("pool", xc[p], zx[p], self.cw[:, c, 1:2], self.cb[:, c:c + 1], ALU.mult, ALU.add, zxk + ["params"], [("xc", p)])

        def stage_A1b(c):
            p = c % 2
            zxk = [("zx", p, tt) for tt in range(TT)]
            for (k, lo_o, hi_o, lo_i, hi_i) in ((0, 1, L, 0, L - 1), (2, 0, L - 1, 1, L), (3, 0, L - 2, 2, L)):
                self.stt(seg(xc[p], lo_o, hi_o), seg(zx[p], lo_i, hi_i), self.cw[:, c, k:k + 1], seg(xc[p], lo_o, hi_o),
                         ALU.mult, ALU.add, zxk + [("xc", p), "params"], [("xc", p)])
            self.copy("dve", xcb[p], xc[p], [("xc", p)], [("xcb", p)])

        def stage_A2(c):
            p = c % 2
            for tt in range(TT):
                ts_ = slice(tt * 512, (tt + 1) * 512)
                if tt >= NHEAD:
                    mm_zg(c, tt)
                self.act(gz[p][:, ts_], self.PS(2 + tt % 2), AF.Gelu_apprx_tanh, [("ps", 2 + tt % 2)], [("gz", p, tt)])

        def stage_B(c):
            p = c % 2
            for tt in range(TT):
                ts_ = slice(tt * 512, (tt + 1) * 512)
                for d in range(2):
                    br, bi = 4 + 2 * d, 5 + 2 * d
                    self.mms([(self.PS(br), self.BD[(0, d)][:, c, :], xcb[p][:, ts_], True, True)],
                             [("BD", 0, d), ("xcb", p)], [("ps", br)])
                    self.mms([(self.PS(bi), self.BD[(1, d)][:, c, :], xcb[p][:, ts_], True, True)],
                             [("BD", 1, d), ("xcb", p)], [("ps", bi)])
                for d in range(2):
                    br, bi = 4 + 2 * d, 5 + 2 * d
                    self.act(a_[d][:, ts_], self.PS(br), AF.Tanh, [("ps", br), "hpar"], [("a", d)],
                             scale=0.5, bias=self.hba[:, d, c:c + 1])
                    self.act(iu[d][:, ts_], self.PS(bi), AF.Tanh, [("ps", bi), "hpar"], [("iu", d)],
                             scale=0.5, bias=self.hbx[:, d, c:c + 1])

        def stage_C(c, part):
            p = c % 2
            if part == 0:
                stage_C0(c)
            else:
                stage_C1(c)

        def stage_Cpre(c):
            p = c % 2
            for d in range(2):
                self.stt(iu[d], iu[d], 1.0, xc[p], ALU.add, ALU.mult, [("iu", d), ("xc", p)], [("iu", d)])

        def stage_C0(c):
            p = c % 2
            for d in range(2):
                self.act(a_[d], a_[d], AF.Exp, [("a", d), "hpar"], [("a", d)],
                         scale=self.hcL[:, d, c:c + 1], bias=self.hcL[:, d, c:c + 1])
                self.act(s_[d], a_[d], AF.Square, [("a", d)], [("s", d)])
                self.act(s_[d], s_[d], AF.Ln, [("s", d)], [("s", d)], scale=-1.0, bias=1.0)
                self.act(s_[d], s_[d], AF.Exp, [("s", d)], [("s", d)], scale=0.5, bias=LN_HALF)

        def stage_C1(c, dirs=(0, 1)):
            p = c % 2
            for d in dirs:
                self.tt("dve", iu[d], iu[d], s_[d], ALU.mult, [("iu", d), ("s", d)], [("iu", d)])
                for si, (st0, ln) in enumerate(job.seqs):
                    sl = slice(st0, st0 + ln)
                    init = self.h0[:, d, c:c + 1] if job.state_in else 0.0
                    f = (lambda x: x) if d == 0 else rev
                    S.add("dve", lambda e, o=f(s_[d][:, sl]), d0=f(a_[d][:, sl]), d1=f(iu[d][:, sl]), init=init:
                          e.tensor_tensor_scan(out=o, data0=d0, data1=d1, initial=init, op0=ALU.mult, op1=ALU.add),
                          [("a", d), ("iu", d), "params", ("s", d)], [("s", d)])
                    if job.state_out:
                        pos = st0 + ln - 1 if d == 0 else st0
                        self.copy("pool", lruo[:, si, d, c:c + 1], s_[d][:, pos:pos + 1], [("s", d)], ["lruo"])

        def stage_C2(c):
            p = c % 2
            self.tt("dve", s_[0], s_[0], s_[1], ALU.add, [("s", 0), ("s", 1)], [("s", 0)])
            yt = ya[p]
            gzk = [("gz", p, tt) for tt in range(TT)]
            if job.perm:
                o = yt.rearrange("p (w r) -> p w r", r=32)
                i0 = s_[0].rearrange("p (r w) -> p w r", w=64)
                i1 = gz[p].rearrange("p (r w) -> p w r", w=64)
            else:
                o, i0, i1 = yt, s_[0], gz[p]
            self.tt("dve", o, i0, i1, ALU.mult, [("s", 0)] + gzk, [("ya", p)])
            self.dma("sp", [(job.YAd[c], yt)], [("ya", p)], [("YAd", job.name)], ("ya", p))

        stage_A_head(0)
        stage_A1(0)
        stage_A1b(0)
        stage_A2(0)
        stage_A_head(1)
        stage_B(0)
        for c in range(8):
            nx = c + 1 < 8
            stage_Cpre(c)
            if nx:
                stage_A1(c + 1)
            stage_C0(c)
            stage_C1(c, (0,))
            if nx:
                stage_A1b(c + 1)
            stage_C1(c, (1,))
            stage_C2(c)
            if nx:
                stage_A2(c + 1)
            if c + 2 < 8:
                stage_A_head(c + 2)
            if nx:
                stage_B(c + 1)
        if job.state_out:
            n = nseq * 16
            lflat = lruo.rearrange("p s d c -> p (s d c)")
            pst = self.PS(0)[0:n, 0:128]
            identf = self.identf
            S.add("pe", lambda e: e.transpose(out=pst, in_=lflat, identity=identf), ["lruo", "identf"], [("ps", 0)])
            lrT = A.f32(128, parts=n)
            self.copy("act", lrT, pst, [("ps", 0)], ["lrT"])
            self.dma("sp", [(T["nsl"].rearrange("s d (c p) -> (s d c) p", p=128), lrT)], ["lrT"], ["nsl"], "nsl")

    def phase_G(self, job):
        A, T, S = self.A, self.T, self.S
        NT = job.NT
        TT = NT // 512
        NTI = NT // 128
        H = self.Hbuf
        VT = A.bf(NTI * 1024).rearrange("p (i v) -> p i v", i=NTI)
        SBst = A.bf(NTI * 1024).rearrange("p (i v) -> p i v", i=NTI)
        lrA = [A.bf(NT, parts=17), A.bf(NT, parts=17)]
        Sst = [A.f32(1024), A.f32(1024)]
        Sfb = A.bf(1024)
        lsets = [[A.f32(512), A.f32(512)], [A.f32(512), A.f32(512)]]
        et = A.f32(512)
        ktail = A.bf(512)
        E = [A.f32(512), A.f32(512)]
        Ei = [A.f32(512), A.f32(512)]
        qe = [A.bf(512), A.bf(512)]
        ke = [A.bf(512), A.bf(512)]
        scT = [A.bf(512), A.bf(512)]
        dec = A.f32(4)
        on = A.f32(1024)
        sg = [A.bf(1024), A.bf(1024)]
        ogt = A.bf(1024)
        ogf = A.bf(1024).rearrange("p (c t) -> p c t", c=8)
        junk = A.bf(256)
        stat = A.f32(12)
        lnscale = -0.5 * math.log(DK)
        vs0 = 1
        if job.name == "P":
            self.wslot = self.wslot[:self.NW] + [A.bf(8 * 512).rearrange("p (c n) -> p c n", c=8) for _ in range(2)]
            vs0 = self.NW
        mt = xsl = None
        if job.name == "P" and self.mod_pending:
            mt = self.mod_alloc()
            xsl = A.bf(8 * 512).rearrange("p (c n) -> p c n", c=8)

        def mod_step(bank):
            if mt is not None and self.mod_pending:
                self.mod_tile(mt, self.mod_pending.pop(0), bank, xslot=xsl)
        wsrc = lambda nm, j: T["w_in"][:, OFF[nm] + j * 512: OFF[nm] + (j + 1) * 512]
        for d in range(2):
            self.memset("pool", lrA[d], 1.0, [("lrA", d)])
        for tt in range(TT):
            ts_ = slice(tt * 512, (tt + 1) * 512)
            for d in range(2):
                b = 2 * (tt % 2) + d
                ps = self.PS(b)[0:16, :]
                self.mms([(ps, self.wlr[:, kc, 16 * d:16 * d + 16], H[:, kc, ts_], kc == 0, kc == 7) for kc in range(8)],
                         ["wlr", (("H", tt), 0), (("H", tt), 1)], [("ps", b)])
                self.copy("act", lrA[d][0:16, ts_], ps, [("ps", b)], [("lrA", d)])

        def l_compute(d, i, bank, pp=0):
            ps = self.PS(bank)
            lt, lk = lsets[pp][d], [("l", pp, d)]
            self.mms([(ps, lrA[d][0:17, i * 128:(i + 1) * 128], self.w2aug[d][0:17, :], True, True)],
                     [("lrA", d), ("w2aug", d)], [("ps", bank)])
            self.act(lt, ps, AF.Exp, [("ps", bank)], lk, scale=-1.0)
            self.act(lt, lt, AF.Ln, lk, lk, bias=1.0)

        def tail_kv(d, i, hk, b_k, b_tail, b_dec, b_kv, with_kv=True, pp=0):
            l_ = lsets[pp]
            lkey = ("l", pp, d)
            Wk = self.wslot[0]
            self.mms([(self.PS(b_k), H[:, kc, i * 128:(i + 1) * 128], Wk[:, kc, :], kc == 0, kc == 7) for kc in range(8)],
                     [("w", 0)] + hk, [("ps", b_k)])
            self.mms([(self.PS(b_tail), self.Mt[d], l_[d], True, True)], [lkey, "masks"], [("ps", b_tail)])
            self.mms([(self.PS(b_dec)[:, h:h + 1], l_[d][:, h * 128:(h + 1) * 128], self.negcol, True, True) for h in range(NH)],
                     [lkey, "cval"], [("ps", b_dec)])
            self.act(et, self.PS(b_tail), AF.Exp, [("ps", b_tail)], ["et"])
            self.act(dec, self.PS(b_dec)[:, 0:4], AF.Exp, [("ps", b_dec)], ["dec"])
            self.tt("dve", ktail, self.PS(b_k), et, ALU.mult, [("ps", b_k), "et"], ["ktail"])
            if with_kv:
                kv_mms(i, b_kv)

        def kv_mms(i, b_kv):
            for hp in range(2):
                self.mms([(self.PS(b_kv + hp)[:, (h % 2) * 256:(h % 2 + 1) * 256], ktail[:, h * 128:(h + 1) * 128],
                           VT[:, i, h * 256:(h + 1) * 256], True, True) for h in (2 * hp, 2 * hp + 1)],
                         ["ktail", ("VT", i, hp)], [("ps", b_kv + hp)])

        def state_update(d, b_kv):
            for h in range(NH):
                sl = slice(h * 256, (h + 1) * 256)
                bk = b_kv + h // 2
                self.stt(Sst[d][:, sl], Sst[d][:, sl], dec[:, h:h + 1], self.PS(bk)[:, (h % 2) * 256:(h % 2 + 1) * 256],
                         ALU.mult, ALU.add, [("S", d), "dec", ("ps", bk)], [("S", d)])

        for si, (st0, ln) in enumerate(job.seqs):
            i0 = st0 // 128
            nch = ln // 128
            for d in range(2):
                if job.state_in:
                    self.dma("sp", [(Sst[d].rearrange("p (h v) -> p h v", h=NH), T["st_gla"][d].rearrange("h k v -> k h v"))],
                             (), [("S", d)], ("S", d))
                else:
                    self.memset("pool", Sst[d], 0.0, [("S", d)])
            self.load_w(0, ("w_in", "k", 0), wsrc("k", 0))
            if job.name == "P":
                self.load_w(3, ("w_in", "q", 0), wsrc("q", 0))
                self.load_w(1, ("w_in", "g", 0), wsrc("g", 0))
                self.load_w(2, ("w_in", "g", 1), wsrc("g", 1))
            Wv = [self.load_w(vs0, ("w_in", "v", 0), wsrc("v", 0)), self.load_w(vs0 + 1, ("w_in", "v", 1), wsrc("v", 1))]
            for n in reversed(range(nch)):
                i = i0 + n
                hk = [(("H", i // 4), 0), (("H", i // 4), 1)]
                for j in range(2):
                    self.mms([(self.PS(1 + j), H[:, kc, i * 128:(i + 1) * 128], Wv[j][:, kc, :], kc == 0, kc == 7) for kc in range(8)],
                             [("w", vs0 + j)] + hk, [("ps", 1 + j)])
                self.copy("act", VT[:, i, 0:512], self.PS(1), [("ps", 1)], [("VT", i, 0)])
                self.copy("dve", VT[:, i, 512:1024], self.PS(2), [("ps", 2)], [("VT", i, 1)])
                l_compute(1, i, 3)
                self.copy("act", SBst[:, i, :], Sst[1], [("S", 1)], [("SBst", i)])
                if job.name == "S" and n % 4 != 3:
                    self.convert_some(1)
                tail_kv(1, i, hk, 0, 4, 7, 5)
                state_update(1, 5)
                mod_step(3)
            if job.state_out:
                self.dma("sp", [(T["nsg"][si, 1].rearrange("h k v -> k h v"), Sst[1].rearrange("p (h v) -> p h v", h=NH))],
                         [("S", 1)], ["nsg"], ("nsgo", 1))
            Wq = self.load_w(3, ("w_in", "q", 0), wsrc("q", 0))
            Wk = self.wslot[0]
            Wg = [self.load_w(1, ("w_in", "g", 0), wsrc("g", 0)), self.load_w(2, ("w_in", "g", 1), wsrc("g", 1))]
            self.copy("act", Sfb, Sst[0], [("S", 0)], ["Sfb"])
            def g2_front(n):
                i = i0 + n
                hk = [(("H", i // 4), 0), (("H", i // 4), 1)]
                tsl = slice(i * 128, (i + 1) * 128)
                pp = n % 2
                l_ = lsets[pp]
                for d in range(2):
                    self.mms([(self.PS(3 + d)[:, h * 128:(h + 1) * 128], l_[d][:, h * 128:(h + 1) * 128], self.Mc[d], True, True)
                              for h in range(NH)], [("l", pp, d), "masks"], [("ps", 3 + d)])
                    self.act(E[d], self.PS(3 + d), AF.Exp, [("ps", 3 + d)], [("E", d)], bias=lnscale)
                    self.act(Ei[d], self.PS(3 + d), AF.Exp, [("ps", 3 + d)], [("Ei", d)], scale=-1.0)
                tail_kv(0, i, hk, 1, 2, 0, 1, with_kv=False, pp=pp)
                self.mms([(self.PS(2)[:, h * 128:(h + 1) * 128], Wq[:, kc, h * 128:(h + 1) * 128], H[:, kc, tsl], kc == 0, kc == 7)
                          for h in range(NH) for kc in range(8)], [("w", 3)] + hk, [("ps", 2)])
                self.mms([(self.PS(0)[:, h * 128:(h + 1) * 128], Wk[:, kc, h * 128:(h + 1) * 128], H[:, kc, tsl], kc == 0, kc == 7)
                          for h in range(NH) for kc in range(8)], [("w", 0)] + hk, [("ps", 0)])
                for d in range(2):
                    self.tt("dve", qe[d], self.PS(2), E[d], ALU.mult, [("ps", 2), ("E", d)], [("qe", d)])
                    self.tt("dve", ke[d], self.PS(0), Ei[d], ALU.mult, [("ps", 0), ("Ei", d)], [("ke", d)])
                kv_mms(i, 1)
                state_update(0, 1)
                for d in range(2):
                    self.mms([(self.PS(3 + d)[:, h * 128:(h + 1) * 128], ke[d][:, h * 128:(h + 1) * 128],
                               qe[d][:, h * 128:(h + 1) * 128], True, True) for h in range(NH)],
                             [("qe", d), ("ke", d)], [("ps", 3 + d)])
                    self.tt("dve", scT[d].rearrange("p (h c) -> p h c", h=NH),
                            self.PS(3 + d).rearrange("p (h c) -> p h c", h=NH), bcast_mid(self.SM[d], NH), ALU.mult,
                            [("ps", 3 + d), "masks"], [("scT", d)])
                pp = n % 2
                for j in range(2):
                    b = 3 + j
                    self.mms([(self.PS(b), H[:, kc, tsl], Wg[j][:, kc, :], kc == 0, kc == 7) for kc in range(8)],
                             [("w", 1 + j)] + hk, [("ps", b)])
                    js = slice(j * 512, (j + 1) * 512)
                    sgx = E[j]
                    self.act(sgx, self.PS(b), AF.Exp, [("ps", b)], [("E", j)], scale=-1.0)
                    self.act(sgx, sgx, AF.Ln, [("E", j)], [("E", j)], bias=1.0)
                    self.act(sgx, sgx, AF.Exp, [("E", j)], [("E", j)], scale=-1.0)
                    self.tt("dve", sg[pp][:, js], self.PS(b), sgx, ALU.mult, [("ps", b), ("E", j)], [("sg", pp, j)])
                if n + 1 < nch:
                    l_compute(0, i + 1, 0, (n + 1) % 2)
                    l_compute(1, i + 1, 1, (n + 1) % 2)

            def g2_mid(n):
                i = i0 + n
                for hp in range(2):
                    lst = []
                    for h in (2 * hp, 2 * hp + 1):
                        o = self.PS(5 + hp)[:, (h % 2) * 256:(h % 2 + 1) * 256]
                        hs = slice(h * 128, (h + 1) * 128)
                        vs = slice(h * 256, (h + 1) * 256)
                        lst.append((o, scT[0][:, hs], VT[:, i, vs], True, False))
                        lst.append((o, scT[1][:, hs], VT[:, i, vs], False, False))
                        lst.append((o, qe[0][:, hs], Sfb[:, vs], False, False))
                        lst.append((o, qe[1][:, hs], SBst[:, i, vs], False, True))
                    self.mms(lst, [("scT", 0), ("scT", 1), ("VT", i, hp), ("qe", 0), ("qe", 1), "Sfb", ("SBst", i)],
                             [("ps", 5 + hp)])
                self.copy("act", Sfb, Sst[0], [("S", 0)], ["Sfb"])

            def g2_tail(n):
                i = i0 + n
                hk = [(("H", i // 4), 0), (("H", i // 4), 1)]
                tsl = slice(i * 128, (i + 1) * 128)
                pp = n % 2
                for h in range(NH):
                    o = self.PS(5 + h // 2)[:, (h % 2) * 256:(h % 2 + 1) * 256]
                    self.act(junk, o, AF.Square, [("ps", 5 + h // 2)], ["gjunk", ("oss", h)], accum=stat[:, h:h + 1])
                self.act(stat[:, 4:8], stat[:, 0:4], AF.Ln, [("oss", h) for h in range(NH)], ["osq"], scale=1.0 / DV, bias=EPS)
                self.act(stat[:, 8:12], stat[:, 4:8], AF.Exp, ["osq"], ["ors"], scale=-0.5)
                for h in range(NH):
                    o = self.PS(5 + h // 2)[:, (h % 2) * 256:(h % 2 + 1) * 256]
                    self.stt(on[:, h * 256:(h + 1) * 256], o, stat[:, 8 + h:9 + h], self.gnb, ALU.mult, ALU.mult,
                             [("ps", 5 + h // 2), "ors", "params"], [("on", h)])
                self.tt("pool", ogt, on, sg[pp], ALU.mult, [("on", h) for h in range(NH)] + [("sg", pp, 0), ("sg", pp, 1)], ["ogt"])
                if job.name == "S":
                    self.convert_some(1)

            def g2_tail_p(n):
                i = i0 + n
                tsl = slice(i * 128, (i + 1) * 128)
                pT = self.PSbf(7, 1, 8)
                self.transposes([(pT[:, c, :], ogt[:, c * 128:(c + 1) * 128]) for c in range(8)], ["ogt"], [("ps", 7)])
                self.copy("act", ogf, pT, [("ps", 7)], ["ogf"])
                self.dma("sp", [(job.OGd.rearrange("c p t -> p c t")[:, :, tsl], ogf)], ["ogf"], [("OGd", job.name)], "ogf")
                mod_step(7)

            l_compute(0, i0, 0, 0)
            l_compute(1, i0, 1, 0)
            g2_front(0)
            g2_mid(0)
            for n in range(nch):
                g2_tail(n)
                if n + 1 < nch:
                    g2_front(n + 1)
                g2_tail_p(n)
                if n + 1 < nch:
                    g2_mid(n + 1)
            if job.state_out:
                self.dma("sp", [(T["nsg"][si, 0].rearrange("h k v -> k h v"), Sst[0].rearrange("p (h v) -> p h v", h=NH))],
                         [("S", 0)], ["nsg"], ("nsgo", 0))
        self.wslot = self.wslot[:self.NW]
        for k_ in (self.NW, self.NW + 1):
            self.wname.pop(k_, None)

    def phase_F(self, jobs):
        A, T, S = self.A, self.T, self.S
        self.modA = [A.f32(1024), A.f32(1024)]
        self.modB = [A.f32(1024) for _ in range(4)]
        xt = [A.f32(1024), A.f32(1024)]
        hb = [A.bf(1024), A.bf(1024)]
        tmp = [A.f32(1024), A.f32(1024)]
        junk = A.bf(1024)
        stat = A.f32(56)
        Hg = A.bf(8 * 512).rearrange("p (c t) -> p c t", c=8)
        YAw = A.bf(8 * 512).rearrange("p (c t) -> p c t", c=8)
        OGw = A.bf(8 * 512).rearrange("p (c t) -> p c t", c=8)
        MRG = A.bf(8 * 512).rearrange("p (c t) -> p c t", c=8)
        H2 = A.bf(8 * 512).rearrange("p (c t) -> p c t", c=8)
        HID = A.bf(32 * 512).rearrange("p (c t) -> p c t", c=32)
        x1 = [A.f32(1024) for _ in range(4)]
        sga = [A.f32(512), A.f32(512)]
        t12 = [A.f32(512), A.f32(512)]
        t1 = A.f32(4 * 512).rearrange("p (f t) -> p f t", f=4)
        rl = [A.bf(512), A.bf(512)]
        wslot_ctr = [0]
        HgK = [("Hg", 0), ("Hg", 1)]
        pairs = ((0, 1), (2, 3))

        def rstd(src, rkeys, s, o):
            c = 12 * s + o
            self.act(junk, src, AF.Square, rkeys, ["junk", ("ss", c)], accum=stat[:, c:c + 1])
            self.act(stat[:, c + 1:c + 2], stat[:, c:c + 1], AF.Ln, [("ss", c)], [("sq", c)], scale=1.0 / D, bias=EPS)
            self.act(stat[:, c + 2:c + 3], stat[:, c + 1:c + 2], AF.Exp, [("sq", c)], [("rs", c)], scale=-0.5)
            return stat[:, c + 2:c + 3], ("rs", c)

        def rstd4(src, rkeys, j, o):
            c = 32 + 6 * j + o
            self.act(junk, src, AF.Square, rkeys, ["junk", ("ss", c)], accum=stat[:, c:c + 1])
            self.act(stat[:, c + 1:c + 2], stat[:, c:c + 1], AF.Ln, [("ss", c)], [("sq", c)], scale=1.0 / D, bias=EPS)
            self.act(stat[:, c + 2:c + 3], stat[:, c + 1:c + 2], AF.Exp, [("sq", c)], [("rs", c)], scale=-0.5)
            return stat[:, c + 2:c + 3], ("rs", c)

        def xload(job, i, s):
            rows = self.tile_rows(job, job.x, i, job.perm)
            self.dma("sp", [(xt[s][lo:hi, :], src) for (lo, hi, src) in rows], (), [("xt", s)], ("xt", s))

        xs = xt
        pre_issued = [0]
        pT = self.PSbf(4, 4, 8)
        pbanks = [("ps", 4 + k) for k in range(4)]

        def s1_elem(job, gi, pr):
            sc1, b1 = self.modA
            for j in pr:
                rows = self.tile_rows(job, job.x, gi * 4 + j, job.perm)
                self.dma("sp", [(xs[j % 2][lo:hi, :], src) for (lo, hi, src) in rows], (), [("xt", j % 2)], ("xt", j % 2))
            rs = {}
            for j in pr:
                rs[j] = rstd(xs[j % 2], [("xt", j % 2)], j % 2, 0)
            for j in pr:
                s = j % 2
                self.stt(tmp[s], xs[s], rs[j][0], sc1, ALU.mult, ALU.mult, [("xt", s), rs[j][1], "modA"], [("tmp", s)])
            for j in pr:
                s = j % 2
                self.tt("pool", hb[s], tmp[s], b1, ALU.add, [("tmp", s), "modA"], [("hb", s)])

        def s1_T(pr):
            for j in pr:
                s = j % 2
                self.transposes([(pT[:, c, j * 128:(j + 1) * 128], hb[s][:, c * 128:(c + 1) * 128]) for c in range(8)],
                                [("hb", s)], pbanks)

        def s1_fin(job, gi):
            self.copy("act", Hg[:, 0:4, :], pT[:, 0:4, :], pbanks, [HgK[0]])
            self.copy("dve", Hg[:, 4:8, :], pT[:, 4:8, :], pbanks, [HgK[1]])
            gsl = slice(gi * 512, (gi + 1) * 512)
            self.dma("sp", [(YAw, job.YAd.rearrange("c p t -> p c t")[:, :, gsl]),
                            (OGw, job.OGd.rearrange("c p t -> p c t")[:, :, gsl])],
                     [("YAd", job.name), ("OGd", job.name)], ["YOw"], "YOw")

        groups = [(job, gi) for job in jobs for gi in range(job.NT // 512)]
        g1n, sc2, b2, g2n = self.modB
        for t_, (job, gi) in enumerate(groups):
            if True:
                NG = job.NT // 512
                if t_ == 0:
                    self.build_mod(job, (0, 1))
                    for pr in pairs:
                        s1_elem(job, 0, pr)
                        s1_T(pr)
                    s1_fin(job, 0)
                elif gi == 0:
                    self.build_mod(job, (1,))
                specs = self.f_specs()
                widx = [0]
                issued = [0]
                base = wslot_ctr[0]
                nxt = t_ + 1 < len(groups)
                njob, ngi = groups[t_ + 1] if nxt else (None, None)

                def prefetch(upto):
                    while issued[0] < min(upto, len(specs)):
                        k = issued[0]
                        sl_ = (base + k) % self.NW
                        self.wname[sl_] = specs[k][0] + (job.name, gi)
                        self.dma("sp", [(self.wslot[sl_], T["WS"][k])], [("WS", k)], [("w", sl_)], ("wf", sl_))
                        issued[0] += 1

                def nextw():
                    k = widx[0]
                    prefetch(k + self.NW - 1)
                    widx[0] += 1
                    sl = (base + k) % self.NW
                    return self.wslot[sl], ("w", sl)

                issued[0] = pre_issued[0]
                pre_issued[0] = 0
                prefetch(self.NW - 1)
                for blk in range(2):
                    for br, (rhsY, rkY) in enumerate(((YAw, ["YOw"]), (OGw, ["YOw"]))):
                        WY, kY = nextw()
                        WG, kG = nextw()
                        for half in range(2):
                            fls = (2 * half, 2 * half + 1)
                            for q, fl in enumerate(fls):
                                fs = slice(fl * 128, (fl + 1) * 128)
                                bk = 4 * half + q
                                self.mms([(self.PS(bk), WY[:, kc, fs], rhsY[:, kc, :], kc == 0, kc == 7) for kc in range(8)],
                                         [kY] + rkY, [("ps", bk)], name="brY")
                            for q, fl in enumerate(fls):
                                fs = slice(fl * 128, (fl + 1) * 128)
                                bk = 4 * half + 2 + q
                                self.mms([(self.PS(bk), WG[:, kc, fs], Hg[:, kc, :], kc == 0, kc == 7) for kc in range(8)],
                                         [kG] + HgK, [("ps", bk)], name="brG")
                            for q, fl in enumerate(fls):
                                self.act(sga[q], self.PS(4 * half + 2 + q), AF.Sigmoid, [("ps", 4 * half + 2 + q)], [("sga", q)])
                            for q, fl in enumerate(fls):
                                f = blk * 4 + fl
                                bk = 4 * half + q
                                if br == 0:
                                    self.tt("dve", t1[:, fl, :], self.PS(bk), sga[q], ALU.mult, [("ps", bk), ("sga", q)], [("t1", fl)])
                                else:
                                    self.tt("dve", t12[q], self.PS(bk), sga[q], ALU.mult, [("ps", bk), ("sga", q)], [("t12", q)])
                                    self.tt("pool", MRG[:, f, :], t1[:, fl, :], t12[q], ALU.add, [("t1", fl), ("t12", q)], ["MRG"])
                Wo0, kO0 = nextw()
                Wo1, kO1 = nextw()
                for pr in pairs:
                    for j in pr:
                        for cg, (Wo, kO) in enumerate(((Wo0, kO0), (Wo1, kO1))):
                            bk = 2 * j + cg
                            self.mms([(self.PS(bk), MRG[:, kc, j * 128:(j + 1) * 128], Wo[:, kc, :], kc == 0, kc == 7) for kc in range(8)],
                                     [kO, "MRG"], [("ps", bk)], name="mo")
                tmp4 = [tmp[0], tmp[1], t1[:, 0:2, :].rearrange("p f t -> p (f t)"), t1[:, 2:4, :].rearrange("p f t -> p (f t)")]
                tk4 = [[("tmp", 0)], [("tmp", 1)], [("t1", 0), ("t1", 1)], [("t1", 2), ("t1", 3)]]
                hb4 = [hb[0], hb[1], t12[0].bitcast(BF16), t12[1].bitcast(BF16)]
                hk4 = [[("hb", 0)], [("hb", 1)], [("t12", 0)], [("t12", 1)]]
                rs = {}
                rs2 = {}

                def m_stage(st_, j):
                    s_ = j % 2
                    if st_ == 0:
                        rs[j] = rstd4(self.PS(2 * j, 2), [("ps", 2 * j), ("ps", 2 * j + 1)], j, 0)
                    elif st_ == 1:
                        self.stt(tmp4[j], self.PS(2 * j, 2), rs[j][0], g1n, ALU.mult, ALU.mult,
                                 [("ps", 2 * j), ("ps", 2 * j + 1), rs[j][1], "modB"], tk4[j])
                    elif st_ == 2:
                        xload(job, gi * 4 + j, s_)
                        self.tt(("dve", "pool")[s_], x1[j], tmp4[j], xt[s_], ALU.add, tk4[j] + [("xt", s_)], [("x1", j)])
                    elif st_ == 3:
                        rs2[j] = rstd4(x1[j], [("x1", j)], j, 3)
                    elif st_ == 4:
                        self.stt(tmp4[j], x1[j], rs2[j][0], sc2, ALU.mult, ALU.mult, [("x1", j), rs2[j][1], "modB"], tk4[j])
                    elif st_ == 5:
                        self.tt(("dve", "pool")[s_], hb4[j], tmp4[j], b2, ALU.add, tk4[j] + ["modB"], hk4[j])
                    elif st_ == 6:
                        pT2 = self.PSbf(2 * j, 1, 8)
                        self.transposes([(pT2[:, c, :], hb4[j][:, c * 128:(c + 1) * 128]) for c in range(8)], hk4[j], [("ps", 2 * j)])
                    elif st_ == 7:
                        pT2 = self.PSbf(2 * j, 1, 8)
                        self.copy(("act", "dve")[s_], H2[:, :, j * 128:(j + 1) * 128], pT2, [("ps", 2 * j)], [("H2", j)])

                for step in range(8 + 3):
                    for j in range(4):
                        st_ = step - j
                        if 0 <= st_ < 8:
                            m_stage(st_, j)
                if nxt:
                    if njob is not job:
                        self.build_mod(njob, (0,))
                    s1_elem(njob, ngi, pairs[0])
                for wi in range(8):
                    W1, k1 = nextw()
                    for fl in range(4):
                        jf = wi * 4 + fl
                        bk = jf % 4
                        self.mms([(self.PS(bk), W1[:, kc, fl * 128:(fl + 1) * 128], H2[:, kc, :], kc == 0, kc == 7) for kc in range(8)],
                                 [k1] + [("H2", j_) for j_ in range(4)], [("ps", bk)], name="mlp1")
                        q = jf % 2
                        self.act(rl[q], self.PS(bk), AF.Relu, [("ps", bk)], [("rl", q)])
                        self.tt("pool", HID[:, jf, :], rl[q], rl[q], ALU.mult, [("rl", q)], [("HID", jf)])
                    if nxt and wi == 3:
                        s1_T(pairs[0])
                        s1_elem(njob, ngi, pairs[1])
                    if nxt and wi == 7:
                        s1_T(pairs[1])
                        s1_fin(njob, ngi)
                hidk = [("HID", jf) for jf in range(32)]
                for kg in range(4):
                    for cg in range(2):
                        W2, k2 = nextw()
                        for j in range(4):
                            bk = 2 * j + cg
                            self.mms([(self.PS(bk), HID[:, kg * 8 + kc, j * 128:(j + 1) * 128], W2[:, kc, :],
                                       kg == 0 and kc == 0, kg == 3 and kc == 7) for kc in range(8)],
                                     [k2] + hidk[kg * 8:(kg + 1) * 8], [("ps", bk)], name="mlp2")
                if nxt:
                    nbase = (base + len(specs)) % self.NW
                    for k in range(self.NW - 1):
                        sl_ = (nbase + k) % self.NW
                        self.wname[sl_] = specs[k][0] + ("pre", job.name, gi)
                        self.dma("sp", [(self.wslot[sl_], T["WS"][k])], [("WS", k)], [("w", sl_)], ("wf", sl_))
                    pre_issued[0] = self.NW - 1
                for pr in pairs:
                    rs = {}
                    for j in pr:
                        rs[j] = rstd(self.PS(2 * j, 2), [("ps", 2 * j), ("ps", 2 * j + 1)], j % 2, 9)
                    for j in pr:
                        s = j % 2
                        self.stt(tmp[s], self.PS(2 * j, 2), rs[j][0], g2n, ALU.mult, ALU.mult,
                                 [("ps", 2 * j), ("ps", 2 * j + 1), rs[j][1], "modB"], [("tmp", s)])
                    for j in pr:
                        s = j % 2
                        self.tt(("dve", "pool")[s], xt[s], tmp[s], x1[j], ALU.add, [("tmp", s), ("x1", j)], [("xt", s)])
                    for j in pr:
                        s = j % 2
                        rows = self.tile_rows(job, job.y, gi * 4 + j, job.perm)
                        self.dma("sp", [(dst, xt[s][lo:hi, :]) for (lo, hi, dst) in rows], [("xt", s)], ["yout"], ("xo", s))
                wslot_ctr[0] = (base + len(specs)) % self.NW

    def f_specs(self):
        T = self.T
        specs = []
        for blk in range(2):
            cs = slice(blk * 512, (blk + 1) * 512)
            specs += [(("lru_up", blk), T["lru_up"][:, cs]),
                      (("w_in", "ga", blk), T["w_in"][:, OFF["ga"] + blk * 512: OFF["ga"] + (blk + 1) * 512]),
                      (("gla_up", blk), T["gla_up"][:, cs]),
                      (("w_in", "gb", blk), T["w_in"][:, OFF["gb"] + blk * 512: OFF["gb"] + (blk + 1) * 512])]
        specs += [(("w_out", cg), T["w_out"][:, cg * 512:(cg + 1) * 512]) for cg in range(2)]
        specs += [(("w1", wi), T["mlp_w1"][:, wi * 512:(wi + 1) * 512]) for wi in range(8)]
        specs += [(("w2", kg, cg), T["mlp_w2"][kg * 1024:(kg + 1) * 1024, cg * 512:(cg + 1) * 512])
                  for kg in range(4) for cg in range(2)]
        return specs

    def convert_some(self, n):
        for _ in range(n):
            if not self.conv_pending:
                return
            k, src = self.conv_pending.pop(0)
            self.dma("pool", [(self.T["WS"][k], src.rearrange("(c p) n -> p c n", p=128))], (), [("WS", k)], "WSconv")

    def rms_stat(self, src, rkeys, junk, stat, o, n):
        self.act(junk[:, 0:n], src, AF.Square, rkeys, ["junk", ("ss", o)], accum=stat[:, o:o + 1])
        self.act(stat[:, o + 1:o + 2], stat[:, o:o + 1], AF.Ln, [("ss", o)], [("sq", o)], scale=1.0 / n, bias=EPS)
        self.act(stat[:, o + 2:o + 3], stat[:, o + 1:o + 2], AF.Exp, [("sq", o)], [("rs", o)], scale=-0.5)


_CACHE = {}


def build_program():
    nc = bass.Bass("TRN2", target_bir_lowering=False)
    T = {}

    def inp(name, shape):
        T[name] = nc.dram_tensor(name, list(shape), F32, kind="ExternalInput").ap()

    def outp(name, shape):
        T[name] = nc.dram_tensor(name, list(shape), F32, kind="ExternalOutput").ap()

    inp("xp", (512, D))
    inp("xs", (2048, D))
    inp("st_lru", (2, D))
    inp("st_gla", (2, NH, DK, DV))
    inp("cvec", (2, D))
    inp("w_mod", (D, 6 * D))
    inp("b_mod", (6 * D,))
    inp("norm_g", (4, D))
    inp("w_in", (D, 7200))
    inp("conv_w", (4, D))
    inp("conv_b", (D,))
    inp("lru_wa", (2, 16, 64, 64))
    inp("lru_ba", (2, D))
    inp("lru_wx", (2, 16, 64, 64))
    inp("lru_bx", (2, D))
    inp("lru_L", (2, D))
    inp("lru_up", (D, D))
    inp("gla_w2", (2, 16, 512))
    inp("gla_b2", (2, 512))
    inp("gla_norm_g", (DV,))
    inp("gla_up", (D, D))
    inp("w_out", (D, D))
    inp("mlp_w1", (D, DFF))
    inp("mlp_w2", (DFF, D))
    outp("yp", (512, D))
    outp("ys", (2048, D))
    outp("nsl", (2, 2, D))
    outp("nsg", (2, 2, NH, DK, DV))
    kind = dict(kind="ExternalOutput") if DEBUG else {}
    T["modd"] = nc.dram_tensor("modd", [2, 6 * D], F32, **kind).ap()
    T["WS"] = nc.dram_tensor("WS", [26, 128, 8, 512], BF16).ap()
    T["YAd_P"] = nc.dram_tensor("YAd_P", [8, 128, 512], BF16, **kind).ap()
    T["OGd_P"] = nc.dram_tensor("OGd_P", [8, 128, 512], BF16, **kind).ap()
    T["YAd_S"] = nc.dram_tensor("YAd_S", [8, 128, 2048], BF16, **kind).ap()
    T["OGd_S"] = nc.dram_tensor("OGd_S", [8, 128, 2048], BF16, **kind).ap()
    B = Builder(nc)
    with nc.allow_non_contiguous_dma(reason="small strided parameter / state layouts"):
        stats = B.build(T)
    return nc, stats, B


def make_in_maps(inputs):
    f = lambda a: np.ascontiguousarray(np.asarray(a, dtype=np.float32))
    x_prompt = f(inputs["x_prompt"])
    x_sample = f(inputs["x_sample"])
    state_lru = f(inputs["state_lru"])
    state_gla = f(inputs["state_gla"])
    c = f(inputs["c"])
    c_ctx = f(inputs["c_ctx"])
    shared = {
        "w_mod": f(inputs["w_mod"])[0], "b_mod": f(inputs["b_mod"])[0], "norm_g": f(inputs["norm_g"])[0],
        "w_in": f(inputs["w_in"])[0], "conv_w": f(inputs["conv_w"])[0], "conv_b": f(inputs["conv_b"])[0],
        "lru_wa": f(inputs["lru_wa"])[0], "lru_ba": f(inputs["lru_ba"])[0], "lru_wx": f(inputs["lru_wx"])[0],
        "lru_bx": f(inputs["lru_bx"])[0], "lru_L": f(inputs["lru_L"])[0], "lru_up": f(inputs["lru_up"])[0],
        "gla_w2": f(inputs["gla_w2"])[0], "gla_b2": f(inputs["gla_b2"])[0], "gla_norm_g": f(inputs["gla_norm_g"])[0],
        "gla_up": f(inputs["gla_up"])[0], "w_out": f(inputs["w_out"])[0], "mlp_w1": f(inputs["mlp_w1"])[0],
        "mlp_w2": f(inputs["mlp_w2"])[0],
    }
    in_maps = []
    for k in range(NCORES):
        m = dict(shared)
        m["xp"] = np.ascontiguousarray(x_prompt[2 * k:2 * k + 2].reshape(512, D))
        m["xs"] = np.ascontiguousarray(x_sample[k])
        m["st_lru"] = np.ascontiguousarray(state_lru[k, 0])
        m["st_gla"] = np.ascontiguousarray(state_gla[k, 0])
        m["cvec"] = np.ascontiguousarray(np.stack([c_ctx, c[k]], axis=0))
        in_maps.append(m)
    return in_maps


def kernel(**inputs):
    if "nc" not in _CACHE:
        _CACHE["nc"] = build_program()[0]
    nc = _CACHE["nc"]
    in_maps = make_in_maps(inputs)
    res = run_bass_kernel_spmd(nc, in_maps, core_ids=list(range(NCORES)))
    rs = res.results
    y_prompt = np.concatenate([np.asarray(r["yp"]).reshape(2, 256, D) for r in rs], axis=0).astype(np.float32)
    y_sample = np.stack([np.asarray(r["ys"]) for r in rs], axis=0).astype(np.float32)
    nsl = np.concatenate([np.asarray(r["nsl"]).reshape(2, 1, 2, D) for r in rs], axis=0).astype(np.float32)
    nsg = np.concatenate([np.asarray(r["nsg"]).reshape(2, 1, 2, NH, DK, DV) for r in rs], axis=0).astype(np.float32)
    return (y_prompt, y_sample, nsl, nsg)
```
